# Optimizing a Trainium2 kernel written in Bass

```python
import math
import jax, jax.numpy as jnp
from jax import lax
import numpy as np

D_MODEL = 2048
BATCH = 1
SEQ = 16384
DEPTH = 1
DEC_BATCH = 4
DEC_SEQ = 4096
PAST_LEN = 128

MLA_HEADS = 8
MLA_Q_LORA = 768
MLA_KV_LORA = 512
MLA_NOPE = 128
MLA_ROPE = 64
MLA_V = 128
DIFF_HEADS = 8
DIFF_QK = 64
DIFF_V = 2 * DIFF_QK
DIFF_ROT = DIFF_QK // 4
D_FF = 4 * D_MODEL
ROPE_THETA = 500000.0
Q_BLOCK = 128
LN_EPS = 1e-5
RMS_EPS = 1e-6
DN_ALPHA = (2.0 * DEPTH) ** 0.25
DN_BETA = (8.0 * DEPTH) ** -0.25

C_QA = MLA_Q_LORA
C_KVA = MLA_KV_LORA + MLA_ROPE
C_DQ = DIFF_HEADS * 2 * DIFF_QK
C_DK = DIFF_HEADS * 2 * DIFF_QK
C_DV = DIFF_HEADS * DIFF_V
C_GATE = 2 * D_MODEL
C_IN = C_QA + C_KVA + C_DQ + C_DK + C_DV + C_GATE
IN_SPLITS = (C_QA, C_QA + C_KVA, C_QA + C_KVA + C_DQ, C_QA + C_KVA + C_DQ + C_DK,
             C_QA + C_KVA + C_DQ + C_DK + C_DV)

kernel_name = 'hybrid_mla_diffattn_gated_encoder'


def _layernorm(x, g, b):
    xf = x.astype(jnp.float32)
    mu = jnp.mean(xf, -1, keepdims=True)
    var = jnp.mean(jnp.square(xf - mu), -1, keepdims=True)
    y = (xf - mu) * lax.rsqrt(var + LN_EPS) * g.astype(jnp.float32) + b.astype(jnp.float32)
    return y.astype(x.dtype)


def _rmsnorm(x, g):
    xf = x.astype(jnp.float32)
    ms = jnp.mean(jnp.square(xf), -1, keepdims=True)
    return (xf * lax.rsqrt(ms + RMS_EPS) * g.astype(jnp.float32)).astype(x.dtype)


def _rope_tables(seq_len, rot_dim):
    inv_freq = ROPE_THETA ** (-jnp.arange(0, rot_dim, 2, dtype=jnp.float32) / rot_dim)
    ang = jnp.arange(seq_len, dtype=jnp.float32)[:, None] * inv_freq[None, :]
    return jnp.cos(ang), jnp.sin(ang)


def _rope(x, cos, sin):
    half = x.shape[-1] // 2
    shp = (x.shape[1],) + (1,) * (x.ndim - 3) + (half,)
    c, s = cos.reshape(shp), sin.reshape(shp)
    xf = x.astype(jnp.float32)
    x1, x2 = xf[..., :half], xf[..., half:]
    return jnp.concatenate([x1 * c - x2 * s, x2 * c + x1 * s], axis=-1).astype(x.dtype)


def _partial_rope(x, cos, sin):
    return jnp.concatenate([_rope(x[..., :DIFF_ROT], cos, sin), x[..., DIFF_ROT:]], axis=-1)


def _blocks(t):
    b, s = t.shape[0], t.shape[1]
    return jnp.moveaxis(t.reshape((b, s // Q_BLOCK, Q_BLOCK) + t.shape[2:]), 1, 0)


def _unblocks(t):
    nb, b, q = t.shape[0], t.shape[1], t.shape[2]
    return jnp.moveaxis(t, 0, 1).reshape((b, nb * q) + t.shape[3:])


def _mla_attention(q_nope, q_rope, k_nope, k_rope, v):
    scale = (MLA_NOPE + MLA_ROPE) ** -0.5

    def block(qs):
        qn, qr = qs
        s = (jnp.einsum('bqhd,bkhd->bhqk', qn, k_nope)
             + jnp.einsum('bqhr,bkr->bhqk', qr, k_rope)).astype(jnp.float32) * scale
        p = jax.nn.softmax(s, axis=-1)
        return jnp.einsum('bhqk,bkhd->bqhd', p.astype(v.dtype), v)

    return _unblocks(lax.map(block, (_blocks(q_nope), _blocks(q_rope))))


def _diff_attention(q, k, v, lam):
    scale = DIFF_QK ** -0.5

    def block(qb):
        s = jnp.einsum('bqhmd,bkhmd->bhmqk', qb, k).astype(jnp.float32) * scale
        p = jax.nn.softmax(s, axis=-1)
        a = p[:, :, 0] - lam * p[:, :, 1]
        return jnp.einsum('bhqk,bkhd->bqhd', a.astype(v.dtype), v)

    return _unblocks(lax.map(block, _blocks(q)))


def _mixer(x, w_in, b_gate, g_qa, w_qb, g_kva, w_kvb, lam_q, lam_k, g_sub,
           w_br_mla, w_br_diff, w_out, lam_init):
    B, S, _ = x.shape
    proj = x @ w_in
    qa, kva, dq, dk, dv, gl = jnp.split(proj, IN_SPLITS, axis=-1)

    cos_m, sin_m = _rope_tables(S, MLA_ROPE)
    q = (_rmsnorm(qa, g_qa) @ w_qb).reshape(B, S, MLA_HEADS, MLA_NOPE + MLA_ROPE)
    q_nope = q[..., :MLA_NOPE]
    q_rope = _rope(q[..., MLA_NOPE:], cos_m, sin_m)
    c_kv = _rmsnorm(kva[..., :MLA_KV_LORA], g_kva)
    k_rope = _rope(kva[..., MLA_KV_LORA:], cos_m, sin_m)
    kv = (c_kv @ w_kvb).reshape(B, S, MLA_HEADS, MLA_NOPE + MLA_V)
    k_nope, v_mla = kv[..., :MLA_NOPE], kv[..., MLA_NOPE:]
    o_mla = _mla_attention(q_nope, q_rope, k_nope, k_rope, v_mla).reshape(B, S, MLA_HEADS * MLA_V)

    cos_d, sin_d = _rope_tables(S, DIFF_ROT)
    dq = _partial_rope(dq.reshape(B, S, DIFF_HEADS, 2, DIFF_QK), cos_d, sin_d)
    dk = _partial_rope(dk.reshape(B, S, DIFF_HEADS, 2, DIFF_QK), cos_d, sin_d)
    dv = dv.reshape(B, S, DIFF_HEADS, DIFF_V)
    lam_dot = jnp.sum(lam_q.astype(jnp.float32) * lam_k.astype(jnp.float32), axis=-1)
    lam = jnp.exp(lam_dot[0]) - jnp.exp(lam_dot[1]) + lam_init
    o_diff = _diff_attention(dq, dk, dv, lam)
    o_diff = (_rmsnorm(o_diff, g_sub) * (1.0 - lam_init)).reshape(B, S, DIFF_HEADS * DIFF_V)

    g_logit = gl + b_gate
    g_mla = jax.nn.sigmoid(g_logit[..., :D_MODEL])
    g_diff = jax.nn.sigmoid(g_logit[..., D_MODEL:])
    merged = g_mla * (o_mla @ w_br_mla) + g_diff * (o_diff @ w_br_diff)
    return merged @ w_out


def _trunk(x, w_in, b_gate, g_qa, w_qb, g_kva, w_kvb, lam_q, lam_k, g_sub,
           w_br_mla, w_br_diff, w_out, ln1_g, ln1_b, w_ff1, w_ff2, ln2_g, ln2_b):
    for l in range(DEPTH):
        lam_init = 0.8 - 0.6 * math.exp(-0.3 * l)
        h = _mixer(x, w_in[l], b_gate[l], g_qa[l], w_qb[l], g_kva[l], w_kvb[l],
                   lam_q[l], lam_k[l], g_sub[l], w_br_mla[l], w_br_diff[l], w_out[l], lam_init)
        x = _layernorm(DN_ALPHA * x + h, ln1_g[l], ln1_b[l])
        f = jnp.square(jax.nn.relu(x @ w_ff1[l])) @ w_ff2[l]
        x = _layernorm(DN_ALPHA * x + f, ln2_g[l], ln2_b[l])
    return x


def setup_inputs(seed: int = 0) -> dict:
    key = jax.random.key(seed)
    ks = jax.random.split(key, 24)

    def nrm(k, shape, scale):
        return jax.random.normal(k, shape, dtype=jnp.float32) * scale

    def gain(k, n):
        return 1.0 + nrm(k, (DEPTH, n), 0.02)

    return {
        'x_prompt': nrm(ks[0], (BATCH, SEQ, D_MODEL), 1.0),
        'x_sample': nrm(ks[1], (DEC_BATCH, DEC_SEQ, D_MODEL), 1.0),
        'w_in': nrm(ks[2], (DEPTH, D_MODEL, C_IN), D_MODEL ** -0.5),
        'b_gate': nrm(ks[3], (DEPTH, C_GATE), 0.01),
        'g_qa': gain(ks[4], MLA_Q_LORA),
        'w_qb': nrm(ks[5], (DEPTH, MLA_Q_LORA, MLA_HEADS * (MLA_NOPE + MLA_ROPE)), MLA_Q_LORA ** -0.5),
        'g_kva': gain(ks[6], MLA_KV_LORA),
        'w_kvb': nrm(ks[7], (DEPTH, MLA_KV_LORA, MLA_HEADS * (MLA_NOPE + MLA_V)), MLA_KV_LORA ** -0.5),
        'lam_q': nrm(ks[8], (DEPTH, 2, DIFF_QK), 0.1),
        'lam_k': nrm(ks[9], (DEPTH, 2, DIFF_QK), 0.1),
        'g_sub': gain(ks[10], DIFF_V),
        'w_br_mla': nrm(ks[11], (DEPTH, MLA_HEADS * MLA_V, D_MODEL), (MLA_HEADS * MLA_V) ** -0.5 * DN_BETA),
        'w_br_diff': nrm(ks[12], (DEPTH, DIFF_HEADS * DIFF_V, D_MODEL), (DIFF_HEADS * DIFF_V) ** -0.5 * DN_BETA),
        'w_out': nrm(ks[13], (DEPTH, D_MODEL, D_MODEL), D_MODEL ** -0.5 * DN_BETA),
        'ln1_g': gain(ks[14], D_MODEL),
        'ln1_b': nrm(ks[15], (DEPTH, D_MODEL), 0.02),
        'w_ff1': nrm(ks[16], (DEPTH, D_MODEL, D_FF), D_MODEL ** -0.5),
        'w_ff2': nrm(ks[17], (DEPTH, D_FF, D_MODEL), D_FF ** -0.5 * DN_BETA),
        'ln2_g': gain(ks[18], D_MODEL),
        'ln2_b': nrm(ks[19], (DEPTH, D_MODEL), 0.02),
    }


def reference(x_prompt, x_sample, w_in, b_gate, g_qa, w_qb, g_kva, w_kvb, lam_q, lam_k, g_sub,
              w_br_mla, w_br_diff, w_out, ln1_g, ln1_b, w_ff1, w_ff2, ln2_g, ln2_b):
    y_prompt = _trunk(x_prompt, w_in, b_gate, g_qa, w_qb, g_kva, w_kvb, lam_q, lam_k, g_sub,
                      w_br_mla, w_br_diff, w_out, ln1_g, ln1_b, w_ff1, w_ff2, ln2_g, ln2_b)
    y_sample = _trunk(x_sample, w_in, b_gate, g_qa, w_qb, g_kva, w_kvb, lam_q, lam_k, g_sub,
                      w_br_mla, w_br_diff, w_out, ln1_g, ln1_b, w_ff1, w_ff2, ln2_g, ln2_b)
    return (y_prompt, y_sample)
```

```python
import math
import os
from contextlib import ExitStack

import numpy as np
import concourse.bass as bass
import concourse.mybir as mybir
from concourse.bass_utils import run_bass_kernel_spmd

F32 = mybir.dt.float32
BF16 = mybir.dt.bfloat16
AF = mybir.ActivationFunctionType
ALU = mybir.AluOpType
AX = mybir.AxisListType

D = 2048
KC = 16
SP_LEN = 16384
SS_LEN = 4096
NK = SP_LEN + SS_LEN
NQ = 4096
TT = 512
NT_K = NK // TT
OWN_TILES = [0, 1, 2, 3, 32, 33, 34, 35]
C_QA, C_KVA, C_DQ, C_DK, C_DV = 768, 576, 1024, 1024, 1024
O_KVA = C_QA
O_DQ = C_QA + C_KVA
O_DK = O_DQ + C_DQ
O_DV = O_DK + C_DK
O_G = O_DV + C_DV
LN_EPS = 1e-5
RMS_EPS = 1e-6
ALPHA = 2.0 ** 0.25
LAM_INIT = 0.8 - 0.6 * math.exp(0.0)
ROPE_THETA = 500000.0
V_BG, V_GQA, V_GKVA, V_L1G, V_L1B, V_L2G, V_L2B, V_GSUB, NV = 0, 32, 38, 42, 58, 74, 90, 106, 107


class Sem:
    def __init__(self, h):
        self.h = h
        self.count = 0


class H:
    __slots__ = ("w", "r")

    def __init__(self):
        self.w = {}
        self.r = {}


class Eng:
    def __init__(self, name, sem, selfwait):
        self.name = name
        self.sem = sem
        self.ops = []
        self.waited = {}
        self.selfwait = selfwait


class Prog:
    def __init__(self, nc, es):
        self.nc = nc
        mk = lambda n: Sem(es.enter_context(nc.semaphore(n)))
        self.E = {
            "pe": Eng("pe", mk("s_pe"), False),
            "act": Eng("act", mk("s_act"), True),
            "dve": Eng("dve", mk("s_dve"), True),
            "pool": Eng("pool", mk("s_pool"), True),
            "sp": Eng("sp", mk("s_sp"), False),
        }
        self.dsems = [mk(f"s_d{i}") for i in range(72)]
        self.dptr = 0
        self.nops = 0

    def newsem(self):
        s = self.dsems[self.dptr % len(self.dsems)]
        self.dptr += 1
        return s

    def op(self, eng, fn, reads=(), writes=(), accw=(), dsem=None, ndma=1):
        E = self.E[eng]
        need = {}

        def add(s, v):
            if v > need.get(s, 0):
                need[s] = v

        for h in reads:
            for s, v in h.w.items():
                add(s, v)
        for h in writes:
            for s, v in h.w.items():
                if s is not E.sem:
                    add(s, v)
            for s, v in h.r.items():
                if s is not E.sem:
                    add(s, v)
        waits = []
        for s, v in need.items():
            if s is E.sem and not E.selfwait:
                continue
            if E.waited.get(s, 0) >= v:
                continue
            E.waited[s] = v
            waits.append((s, v))
        if fn is None:
            E.ops.append((waits, None, None, 0))
            return
        if dsem is not None:
            sem = dsem
            sem.count += 16 * ndma
        else:
            sem = E.sem
            sem.count += 1
        val = sem.count
        E.ops.append((waits, fn, sem, 16 if dsem is not None else 1))
        self.nops += 1
        for h in reads:
            if h.r.get(sem, 0) < val:
                h.r[sem] = val
        for h in writes:
            h.w = {sem: val}
            h.r = {}
        for h in accw:
            h.w[sem] = val

    def dma(self, q, pairs, dsem, reads=(), writes=(), accw=(), **kw):
        def fn(eng, pairs=pairs, kw=kw):
            return [eng.dma_start(out=o, in_=i, **kw) for o, i in pairs]
        self.op(q, fn, reads=reads, writes=writes, accw=accw, dsem=dsem, ndma=len(pairs))

    def flush(self):
        nc = self.nc
        sp = self.E["sp"]
        fw = []
        for s in self.dsems:
            if s.count > 0 and sp.waited.get(s, 0) < s.count:
                sp.waited[s] = s.count
                fw.append((s, s.count))
        sp.ops.append((fw, None, None, 0))
        with nc.Block() as block:
            for name, deco in (("pe", block.tensor), ("act", block.scalar), ("dve", block.vector),
                               ("pool", block.gpsimd), ("sp", block.sync)):
                ops = self.E[name].ops

                def body(eng, ops=ops):
                    for waits, fn, sem, inc in ops:
                        for s, v in waits:
                            eng.wait_ge(s.h, v)
                        if fn is None:
                            continue
                        ins = fn(eng)
                        if isinstance(ins, (list, tuple)):
                            for i_ in ins:
                                i_.then_inc(sem.h, inc)
                        else:
                            ins.then_inc(sem.h, inc)

                deco(body)
        for e in self.E.values():
            e.ops = []
        self.dptr = 0


class Buf:
    def __init__(self, t, P=None, dma=False):
        self.t = t
        self.h = H()
        self.ld = P.newsem() if dma else None


class _T:
    def __init__(self, ap):
        self.ap = ap

    def __getitem__(self, idx):
        return self.ap[idx]


class View:
    def __init__(self, ap):
        self.t = _T(ap)
        self.h = H()


def mm_group(P, out_buf, pieces, reads, extra_writes=()):
    def fn(pe, pieces=pieces):
        n = len(pieces)
        ins = None
        for i, (o, l, r) in enumerate(pieces):
            ins = pe.matmul(o, l, r, start=(i == 0), stop=(i == n - 1))
        return ins
    P.op("pe", fn, reads=reads, writes=[out_buf.h] + list(extra_writes))


def build_program(debug=False, phases="ABCDEF", ntk=NT_K, own=None, egroups=None, eunits=None):
    nc = bass.Bass("TRN2", target_bir_lowering=False)
    es = ExitStack()
    with es:
        def din(name, shape, dt=F32):
            return nc.dram_tensor(name, shape, dt, kind="ExternalInput").ap()

        def dscr(name, shape, dt):
            return nc.dram_tensor(name, shape, dt, kind="ExternalOutput" if debug else "Internal").ap()

        xT = din("xT", [D, NK])
        w_in = din("w_in", [D, 8512])
        w_qb = din("w_qb", [768, 1536])
        w_kvb = din("w_kvb", [512, 2048])
        w_brm = din("w_br_mla", [1024, 2048])
        w_brd = din("w_br_diff", [1024, 2048])
        w_out = din("w_out", [D, D])
        w_ff1 = din("w_ff1", [D, 8192])
        w_ff2 = din("w_ff2", [8192, D])
        vecs_d = din("vecs", [128, NV])
        lamqk_d = din("lamqk", [128, 256])
        ropeM_C = din("ropeM_C", [128, NK])
        ropeM_S = din("ropeM_S", [128, NK])
        ropeD_C = din("ropeD_C", [128, NK])
        ropeD_S = din("ropeD_S", [128, NK])
        perm_d = din("perms", [128, 256])
        yT = nc.dram_tensor("yT", [D, NQ], F32, kind="ExternalOutput").ap()

        NCH = NK // 1024
        KnT = dscr("KnT", [8, 128, NK], BF16)
        KrT = dscr("KrT", [128, NK], BF16)
        Vm = dscr("Vm", [8, NCH, 128, 8, 128], BF16)
        DkT = dscr("DkT", [8, 128, NK], BF16)
        Dv = dscr("Dv", [8, NCH, 128, 8, 128], BF16)
        QnT = dscr("QnT", [8, 128, NQ], BF16)
        QrT = dscr("QrT", [4, 128, NQ], BF16)
        DqT = dscr("DqT", [8, 128, NQ], BF16)
        OmT = dscr("OmT", [8, 128, NQ], BF16)
        OdT = dscr("OdT", [16, 128, NQ], F32)
        Wg = dscr("Wg", [8, 128, 16, 512], BF16)
        Wbm = dscr("Wbm", [2, 128, 8, 1024], BF16)
        Wbd = dscr("Wbd", [2, 128, 8, 1024], BF16)
        Wo = dscr("Wo", [4, 128, 16, 512], BF16)
        W1 = dscr("W1", [16, 128, 16, 512], BF16)
        W2 = dscr("W2", [16, 128, 16, 512], BF16)
        hK = {n: H() for n in ("KnT", "KrT", "Vm", "DkT", "Dv", "QnT", "QrT", "DqT", "OmT", "OdT", "W", "yT")}

        P = Prog(nc, es)

        def sb(stack, name, shape, dt, dma=False):
            return Buf(stack.enter_context(nc.sbuf_tensor("sb_" + name, shape, dt)), P, dma)

        psbig = [es.enter_context(nc.psum_tensor(f"pb{i}", [128, 1024], F32)) for i in range(4)]
        ps = [View(psbig[i // 2][:, (i % 2) * 512:(i % 2 + 1) * 512]) for i in range(8)]
        vecs = sb(es, "vecs", [128, NV], F32, dma=True)
        lamqk = sb(es, "lamqk", [128, 256], F32, dma=True)
        perm32 = sb(es, "perm32", [128, 256], F32, dma=True)
        permb = sb(es, "permb", [128, 256], BF16)
        ones = sb(es, "ones", [128, 128], BF16)
        lamt = sb(es, "lamt", [128, 128], F32)
        lamc = sb(es, "lamc", [128, 8], F32)
        gq = sb(es, "gq", [128, 6], F32)
        gk = sb(es, "gk", [128, 4], F32)
        epsb = sb(es, "epsb", [128, 4], F32)
        EPS_LIST = [512.0 * RMS_EPS, 768.0 * RMS_EPS, 128.0 * RMS_EPS, LN_EPS]
        epsc = {v: epsb.t[:, i:i + 1] for i, v in enumerate(EPS_LIST)}

        P.dma("sp", [(vecs.t[:], vecs_d[:, :])], vecs.ld, writes=[vecs.h])
        P.dma("sp", [(lamqk.t[:], lamqk_d[:, :])], lamqk.ld, writes=[lamqk.h])
        P.dma("sp", [(perm32.t[:], perm_d[:, :])], perm32.ld, writes=[perm32.h])
        P.op("dve", lambda e: e.tensor_copy(out=permb.t[:], in_=perm32.t[:]), reads=[perm32.h], writes=[permb.h])
        P.op("dve", lambda e: e.memset(ones.t[:], 1.0), writes=[ones.h])
        for i_, v_ in enumerate(EPS_LIST):
            P.op("dve", lambda e, i_=i_, v_=v_: e.memset(epsb.t[:, i_:i_ + 1], v_), writes=[epsb.h])
        P.op("dve", lambda e: e.tensor_tensor(out=lamt.t[:], in0=lamqk.t[:, 0:128], in1=lamqk.t[:, 128:256], op=ALU.mult),
             reads=[lamqk.h], writes=[lamt.h])
        P.op("dve", lambda e: e.tensor_reduce(out=lamc.t[:, 0:1], in_=lamt.t[:, 0:64], axis=AX.X, op=ALU.add),
             reads=[lamt.h], writes=[lamc.h])
        P.op("dve", lambda e: e.tensor_reduce(out=lamc.t[:, 1:2], in_=lamt.t[:, 64:128], axis=AX.X, op=ALU.add),
             reads=[lamt.h, lamc.h], writes=[lamc.h])
        P.op("act", lambda e: e.activation(out=lamc.t[:, 2:4], in_=lamc.t[:, 0:2], func=AF.Exp),
             reads=[lamc.h], writes=[lamc.h])
        P.op("dve", lambda e: e.tensor_tensor(out=lamc.t[:, 7:8], in0=lamc.t[:, 2:3], in1=lamc.t[:, 3:4], op=ALU.subtract),
             reads=[lamc.h], writes=[lamc.h])
        P.op("dve", lambda e: e.tensor_scalar(out=lamc.t[:, 5:6], in0=lamc.t[:, 7:8], scalar1=LAM_INIT, scalar2=-1.0,
                                              op0=ALU.add, op1=ALU.mult), reads=[lamc.h], writes=[lamc.h])
        P.op("dve", lambda e: e.tensor_scalar(out=lamc.t[:, 6:7], in0=vecs.t[:, V_GSUB:V_GSUB + 1],
                                              scalar1=(1.0 - LAM_INIT) * math.sqrt(128.0), scalar2=None, op0=ALU.mult),
             reads=[vecs.h, lamc.h], writes=[lamc.h])
        P.op("dve", lambda e: e.tensor_scalar(out=gq.t[:], in0=vecs.t[:, V_GQA:V_GQA + 6], scalar1=math.sqrt(768.0),
                                              scalar2=None, op0=ALU.mult), reads=[vecs.h], writes=[gq.h])
        P.op("dve", lambda e: e.tensor_scalar(out=gk.t[:], in0=vecs.t[:, V_GKVA:V_GKVA + 4], scalar1=math.sqrt(512.0),
                                              scalar2=None, op0=ALU.mult), reads=[vecs.h], writes=[gk.h])

        xT_v = xT.rearrange("(kc p) n -> p kc n", p=128)

        def load_x(xb, T0):
            P.dma("pool", [(xb.t[:], xT_v[:, :, T0:T0 + TT])], xb.ld, writes=[xb.h])

        def rope_fm(src_ap, src_h, src_is_psum, Ct, St, perm_ap, qb, sw, t1, t2, out_ap, out_h):
            xw = [src_h] if src_is_psum else []
            P.op("act", lambda e: e.activation(out=qb.t[:], in_=src_ap, func=AF.Copy), reads=[src_h], writes=[qb.h] + xw)
            mm_group(P, sw, [(sw.t[:], perm_ap, qb.t[:])], reads=[qb.h, permb.h])
            P.op("dve", lambda e: e.tensor_tensor(out=t1.t[:], in0=src_ap, in1=Ct.t[:], op=ALU.mult),
                 reads=[src_h, Ct.h], writes=[t1.h] + xw)
            P.op("dve", lambda e: e.tensor_tensor(out=t2.t[:], in0=sw.t[:], in1=St.t[:], op=ALU.mult),
                 reads=[sw.h, St.h], writes=[t2.h])
            P.op("dve", lambda e: e.tensor_tensor(out=out_ap, in0=t1.t[:], in1=t2.t[:], op=ALU.add),
                 reads=[t1.h, t2.h], writes=[out_h])

        def rsqrt_eps(out_buf, out_ap, in_ap, in_hs, eps):
            P.op("act", lambda e: e.activation(out=out_ap, in_=in_ap, func=AF.Ln, bias=epsc[eps][:, 0:1]), reads=list(in_hs) + [epsb.h],
                 writes=[out_buf.h])
            P.op("act", lambda e: e.activation(out=out_ap, in_=out_ap, func=AF.Exp, scale=-0.5), reads=[out_buf.h], writes=[out_buf.h])

        rr = [0]

        def psn():
            rr[0] += 1
            return ps[rr[0] % 8]

        def phase_A(tiles):
            with ExitStack() as st:
                wA = sb(st, "wA", [128, 16, 640], BF16, dma=True)
                wkvb = sb(st, "wkvb", [128, 4, 2048], BF16)
                wst = [sb(st, f"wstA{i}", [128, 2048], F32, dma=True) for i in range(2)]
                xbs = [sb(st, f"xbA{i}", [128, 16, TT], BF16, dma=True) for i in range(2)]
                ckvb = [sb(st, f"ckvb{i}", [128, 4, TT], BF16) for i in range(2)]
                sq = [sb(st, f"sqA{i}", [128, 4, TT], BF16) for i in range(2)]
                rstdB = [sb(st, f"rstdBA{i}", [128, TT], F32) for i in range(2)]
                rcol = [sb(st, f"rcolA{i}", [128, 4], F32) for i in range(2)]
                knst = [sb(st, f"knst{i}", [128, 8, TT], BF16, dma=True) for i in range(2)]
                vst = [sb(st, f"vst{i}", [128, 8, 4, 128], BF16, dma=True) for i in range(2)]
                krst = [sb(st, f"krst{i}", [128, TT], BF16, dma=True) for i in range(2)]
                Ct = [sb(st, f"CtA{i}", [128, TT], F32, dma=True) for i in range(2)]
                St = [sb(st, f"StA{i}", [128, TT], F32, dma=True) for i in range(2)]
                qb = sb(st, "qbA", [128, TT], BF16)
                t1 = sb(st, "t1A", [128, TT], F32)
                t2 = sb(st, "t2A", [128, TT], F32)
                wA2 = P.newsem()
                P.dma("pool", [(wA.t[:, :, 0:576], w_in.rearrange("(kc p) n -> p kc n", p=128)[:, :, O_KVA:O_KVA + 576])], wA.ld, accw=[wA.h])
                P.dma("pool", [(wA.t[:, :, 576:640], w_in.rearrange("(kc p) n -> p kc n", p=128)[:, :, O_KVA + 512:O_KVA + 576])], wA2, accw=[wA.h])
                wkvb_v = w_kvb.rearrange("(kc p) (h t d) -> kc t p h d", p=128, t=2, d=128)
                for kc in range(4):
                    s_ = wst[kc % 2]
                    P.dma("sp", [(s_.t[:, 0:1024].rearrange("p (h d) -> p h d", d=128), wkvb_v[kc, 0]),
                                 (s_.t[:, 1024:2048].rearrange("p (h d) -> p h d", d=128), wkvb_v[kc, 1])],
                          s_.ld, writes=[s_.h])
                    P.op("act", lambda e, s_=s_, kc=kc: e.activation(out=wkvb.t[:, kc, :], in_=s_.t[:], func=AF.Identity,
                                                                      scale=gk.t[:, kc:kc + 1]),
                         reads=[s_.h, gk.h], writes=[wkvb.h])
                CUT = int(os.environ.get("CUTA", "99"))
                if CUT <= 0:
                    P.flush()
                    return
                load_x(xbs[0], tiles[0] * TT)
                for it, tix in enumerate(tiles):
                    T0 = tix * TT
                    b = it % 2
                    xb = xbs[b]
                    if it + 1 < len(tiles):
                        load_x(xbs[(it + 1) % 2], tiles[it + 1] * TT)
                    P.dma("sp", [(Ct[b].t[:], ropeM_C[:, T0:T0 + TT]), (St[b].t[:], ropeM_S[:, T0:T0 + TT])],
                          Ct[b].ld, writes=[Ct[b].h, St[b].h])
                    if CUT <= 1:
                        continue
                    for oc in range(4):
                        p_ = psn()
                        mm_group(P, p_, [(p_.t[:], wA.t[:, kc, oc * 128:(oc + 1) * 128], xb.t[:, kc, :]) for kc in range(16)],
                                 reads=[wA.h, xb.h])
                        P.op("act", lambda e, p_=p_, oc=oc, b=b: e.activation(out=ckvb[b].t[:, oc, :], in_=p_.t[:], func=AF.Copy),
                             reads=[p_.h], writes=[ckvb[b].h])
                        P.op("act", lambda e, p_=p_, oc=oc, b=b: e.activation(out=sq[b].t[:, oc, :], in_=p_.t[:], func=AF.Square),
                             reads=[p_.h], writes=[sq[b].h])
                    if CUT <= 2:
                        continue
                    p_ = psn()
                    mm_group(P, p_, [(p_.t[:], ones.t[:], sq[b].t[:, oc, :]) for oc in range(4)], reads=[ones.h, sq[b].h])
                    rsqrt_eps(rstdB[b], rstdB[b].t[:], p_.t[:], [p_.h], 512.0 * RMS_EPS)
                    p_ = psn()

                    def fcol(pe, p_=p_, b=b):
                        ins = None
                        for tb in range(4):
                            for oc in range(4):
                                ins = pe.matmul(p_.t[:, tb:tb + 1], sq[b].t[:, oc, tb * 128:(tb + 1) * 128], ones.t[:, 0:1],
                                                start=(oc == 0), stop=(oc == 3))
                        return ins
                    P.op("pe", fcol, reads=[ones.h, sq[b].h], writes=[p_.h])
                    rsqrt_eps(rcol[b], rcol[b].t[:], p_.t[:, 0:4], [p_.h], 512.0 * RMS_EPS)
                    if CUT <= 3:
                        continue
                    for h in range(8):
                        p_ = psn()
                        mm_group(P, p_, [(p_.t[:], wkvb.t[:, c4, h * 128:(h + 1) * 128], ckvb[b].t[:, c4, :]) for c4 in range(4)],
                                 reads=[wkvb.h, ckvb[b].h])
                        P.op("dve", lambda e, p_=p_, h=h, b=b: e.tensor_tensor(out=knst[b].t[:, h, :], in0=p_.t[:], in1=rstdB[b].t[:],
                                                                               op=ALU.mult),
                             reads=[p_.h, rstdB[b].h], writes=[knst[b].h])
                    P.dma("sp", [(KnT.rearrange("h p n -> p h n")[:, :, T0:T0 + TT], knst[b].t[:])], knst[b].ld,
                          reads=[knst[b].h], accw=[hK["KnT"]])
                    if CUT <= 4:
                        continue
                    for tb in range(4):
                        for half in range(2):
                            p_ = psn()
                            mm_group(P, p_, [(p_.t[:], ckvb[b].t[:, c4, tb * 128:(tb + 1) * 128],
                                              wkvb.t[:, c4, 1024 + half * 512:1024 + (half + 1) * 512]) for c4 in range(4)],
                                     reads=[wkvb.h, ckvb[b].h])
                            P.op("act", lambda e, p_=p_, tb=tb, half=half, b=b: e.activation(
                                out=vst[b].t[:, half * 4:(half + 1) * 4, tb, :],
                                in_=p_.t[:].rearrange("p (h d) -> p h d", d=128), func=AF.Identity, scale=rcol[b].t[:, tb:tb + 1]),
                                 reads=[p_.h, rcol[b].h], writes=[vst[b].h])
                    ch, k0 = T0 // 1024, (T0 % 1024) // 128
                    P.dma("sp", [(Vm[:, ch, :, k0:k0 + 4, :].rearrange("h p k d -> p h k d"), vst[b].t[:])], vst[b].ld,
                          reads=[vst[b].h], accw=[hK["Vm"]])
                    if CUT <= 5:
                        continue
                    p_ = psn()
                    mm_group(P, p_, [(p_.t[:], wA.t[:, kc, 512:640], xb.t[:, kc, :]) for kc in range(16)], reads=[wA.h, xb.h])
                    sw = psn()
                    rope_fm(p_.t[:], p_.h, True, Ct[b], St[b], permb.t[:, 0:128], qb, sw, t1, t2, krst[b].t[:], krst[b].h)
                    P.dma("sp", [(KrT[:, T0:T0 + TT], krst[b].t[:])], krst[b].ld, reads=[krst[b].h], accw=[hK["KrT"]])
                P.flush()

        def phase_B(tiles):
            with ExitStack() as st:
                wB = sb(st, "wB", [128, 16, 2048], BF16, dma=True)
                xbs = [sb(st, f"xbB{i}", [128, 16, TT], BF16, dma=True) for i in range(2)]
                dkst = [sb(st, f"dkst{i}", [128, 8, TT], BF16, dma=True) for i in range(2)]
                dvst = [sb(st, f"dvst{i}", [128, 8, 4, 128], BF16, dma=True) for i in range(2)]
                Ct = [sb(st, f"CtB{i}", [128, TT], F32, dma=True) for i in range(2)]
                St = [sb(st, f"StB{i}", [128, TT], F32, dma=True) for i in range(2)]
                qb = [sb(st, f"qbB{i}", [128, TT], BF16) for i in range(2)]
                t1 = [sb(st, f"t1B{i}", [128, TT], F32) for i in range(2)]
                t2 = [sb(st, f"t2B{i}", [128, TT], F32) for i in range(2)]
                P.dma("pool", [(wB.t[:], w_in.rearrange("(kc p) n -> p kc n", p=128)[:, :, O_DK:O_DK + 2048])], wB.ld, writes=[wB.h])
                load_x(xbs[0], tiles[0] * TT)
                for it, tix in enumerate(tiles):
                    T0 = tix * TT
                    b = it % 2
                    xb = xbs[b]
                    if it + 1 < len(tiles):
                        load_x(xbs[(it + 1) % 2], tiles[it + 1] * TT)
                    P.dma("sp", [(Ct[b].t[:], ropeD_C[:, T0:T0 + TT]), (St[b].t[:], ropeD_S[:, T0:T0 + TT])],
                          Ct[b].ld, writes=[Ct[b].h, St[b].h])
                    for h in range(8):
                        p_ = psn()
                        mm_group(P, p_, [(p_.t[:], wB.t[:, kc, h * 128:(h + 1) * 128], xb.t[:, kc, :]) for kc in range(16)],
                                 reads=[wB.h, xb.h])
                        sw = psn()
                        rope_fm(p_.t[:], p_.h, True, Ct[b], St[b], permb.t[:, 128:256], qb[h % 2], sw, t1[h % 2], t2[h % 2],
                                dkst[b].t[:, h, :], dkst[b].h)
                    P.dma("sp", [(DkT.rearrange("h p n -> p h n")[:, :, T0:T0 + TT], dkst[b].t[:])], dkst[b].ld,
                          reads=[dkst[b].h], accw=[hK["DkT"]])
                    for tb in range(4):
                        for half in range(2):
                            p_ = psn()
                            mm_group(P, p_, [(p_.t[:], xb.t[:, kc, tb * 128:(tb + 1) * 128],
                                              wB.t[:, kc, 1024 + half * 512:1024 + (half + 1) * 512]) for kc in range(16)],
                                     reads=[wB.h, xb.h])
                            eng = "act" if (tb + half) % 2 == 0 else "dve"
                            if eng == "act":
                                P.op("act", lambda e, p_=p_, tb=tb, half=half, b=b: e.activation(
                                    out=dvst[b].t[:, half * 4:(half + 1) * 4, tb, :],
                                    in_=p_.t[:].rearrange("p (h d) -> p h d", d=128), func=AF.Copy),
                                     reads=[p_.h], writes=[dvst[b].h])
                            else:
                                P.op("dve", lambda e, p_=p_, tb=tb, half=half, b=b: e.tensor_copy(
                                    out=dvst[b].t[:, half * 4:(half + 1) * 4, tb, :],
                                    in_=p_.t[:].rearrange("p (h d) -> p h d", d=128)),
                                     reads=[p_.h], writes=[dvst[b].h])
                    ch, k0 = T0 // 1024, (T0 % 1024) // 128
                    P.dma("sp", [(Dv[:, ch, :, k0:k0 + 4, :].rearrange("h p k d -> p h k d"), dvst[b].t[:])], dvst[b].ld,
                          reads=[dvst[b].h], accw=[hK["Dv"]])
                P.flush()

        def phase_C(tiles):
            with ExitStack() as st:
                wq = sb(st, "wqa", [128, 16, 768], BF16, dma=True)
                wqb = sb(st, "wqb", [128, 6, 1536], BF16)
                wst = [sb(st, f"wstC{i}", [128, 1536], F32, dma=True) for i in range(2)]
                xbs = [sb(st, f"xbC{i}", [128, 16, TT], BF16, dma=True) for i in range(2)]
                qab = [sb(st, f"qab{i}", [128, 6, TT], BF16) for i in range(2)]
                sq = [sb(st, f"sqC{i}", [128, 6, TT], BF16) for i in range(2)]
                rstdB = [sb(st, f"rstdBC{i}", [128, TT], F32) for i in range(2)]
                qnst = [sb(st, f"qnst{i}", [128, 8, TT], BF16, dma=True) for i in range(2)]
                qrst = [sb(st, f"qrst{i}", [128, 4, TT], BF16, dma=True) for i in range(2)]
                Ct = [sb(st, f"CtC{i}", [128, TT], F32, dma=True) for i in range(2)]
                St = [sb(st, f"StC{i}", [128, TT], F32, dma=True) for i in range(2)]
                qf = [sb(st, f"qfC{i}", [128, TT], F32) for i in range(2)]
                qb = [sb(st, f"qbC{i}", [128, TT], BF16) for i in range(2)]
                t1 = [sb(st, f"t1C{i}", [128, TT], F32) for i in range(2)]
                t2 = [sb(st, f"t2C{i}", [128, TT], F32) for i in range(2)]
                P.dma("pool", [(wq.t[:], w_in.rearrange("(kc p) n -> p kc n", p=128)[:, :, 0:768])], wq.ld, writes=[wq.h])
                wqb_v = w_qb.rearrange("(kc p) (h d) -> kc p h d", p=128, d=192)
                for kc in range(6):
                    s_ = wst[kc % 2]
                    P.dma("sp", [(s_.t[:, 0:1024].rearrange("p (h d) -> p h d", d=128), wqb_v[kc, :, :, 0:128]),
                                 (s_.t[:, 1024:1536].rearrange("p (h d) -> p h d", d=64), wqb_v[kc, :, :, 128:192])],
                          s_.ld, writes=[s_.h])
                    P.op("act", lambda e, s_=s_, kc=kc: e.activation(out=wqb.t[:, kc, :], in_=s_.t[:], func=AF.Identity,
                                                                      scale=gq.t[:, kc:kc + 1]),
                         reads=[s_.h, gq.h], writes=[wqb.h])
                load_x(xbs[0], tiles[0] * TT)
                for it, tix in enumerate(tiles):
                    T0 = tix * TT
                    Q0 = it * TT
                    b = it % 2
                    xb = xbs[b]
                    if it + 1 < len(tiles):
                        load_x(xbs[(it + 1) % 2], tiles[it + 1] * TT)
                    P.dma("sp", [(Ct[b].t[:], ropeM_C[:, T0:T0 + TT]), (St[b].t[:], ropeM_S[:, T0:T0 + TT])],
                          Ct[b].ld, writes=[Ct[b].h, St[b].h])
                    for oc in range(6):
                        p_ = psn()
                        mm_group(P, p_, [(p_.t[:], wq.t[:, kc, oc * 128:(oc + 1) * 128], xb.t[:, kc, :]) for kc in range(16)],
                                 reads=[wq.h, xb.h])
                        P.op("act", lambda e, p_=p_, oc=oc, b=b: e.activation(out=qab[b].t[:, oc, :], in_=p_.t[:], func=AF.Copy),
                             reads=[p_.h], writes=[qab[b].h])
                        P.op("act", lambda e, p_=p_, oc=oc, b=b: e.activation(out=sq[b].t[:, oc, :], in_=p_.t[:], func=AF.Square),
                             reads=[p_.h], writes=[sq[b].h])
                    p_ = psn()
                    mm_group(P, p_, [(p_.t[:], ones.t[:], sq[b].t[:, oc, :]) for oc in range(6)], reads=[ones.h, sq[b].h])
                    rsqrt_eps(rstdB[b], rstdB[b].t[:], p_.t[:], [p_.h], 768.0 * RMS_EPS)
                    for h in range(8):
                        p_ = psn()
                        mm_group(P, p_, [(p_.t[:], wqb.t[:, c6, h * 128:(h + 1) * 128], qab[b].t[:, c6, :]) for c6 in range(6)],
                                 reads=[wqb.h, qab[b].h])
                        P.op("dve", lambda e, p_=p_, h=h, b=b: e.tensor_tensor(out=qnst[b].t[:, h, :], in0=p_.t[:], in1=rstdB[b].t[:],
                                                                               op=ALU.mult),
                             reads=[p_.h, rstdB[b].h], writes=[qnst[b].h])
                    P.dma("sp", [(QnT.rearrange("h p n -> p h n")[:, :, Q0:Q0 + TT], qnst[b].t[:])], qnst[b].ld,
                          reads=[qnst[b].h], accw=[hK["QnT"]])
                    for pr in range(4):
                        p_ = psn()
                        mm_group(P, p_, [(p_.t[:], wqb.t[:, c6, 1024 + pr * 128:1024 + (pr + 1) * 128], qab[b].t[:, c6, :])
                                         for c6 in range(6)], reads=[wqb.h, qab[b].h])
                        f_ = qf[pr % 2]
                        P.op("dve", lambda e, p_=p_, f_=f_, b=b: e.tensor_tensor(out=f_.t[:], in0=p_.t[:], in1=rstdB[b].t[:], op=ALU.mult),
                             reads=[p_.h, rstdB[b].h], writes=[f_.h])
                        sw = psn()
                        rope_fm(f_.t[:], f_.h, False, Ct[b], St[b], permb.t[:, 0:128], qb[pr % 2], sw, t1[pr % 2], t2[pr % 2],
                                qrst[b].t[:, pr, :], qrst[b].h)
                    P.dma("sp", [(QrT.rearrange("h p n -> p h n")[:, :, Q0:Q0 + TT], qrst[b].t[:])], qrst[b].ld,
                          reads=[qrst[b].h], accw=[hK["QrT"]])
                P.flush()

        def phase_D(tiles):
            with ExitStack() as st:
                wd = sb(st, "wdq", [128, 16, 1024], BF16, dma=True)
                xbs = [sb(st, f"xbD{i}", [128, 16, TT], BF16, dma=True) for i in range(2)]
                dqst = [sb(st, f"dqst{i}", [128, 8, TT], BF16, dma=True) for i in range(2)]
                Ct = [sb(st, f"CtD{i}", [128, TT], F32, dma=True) for i in range(2)]
                St = [sb(st, f"StD{i}", [128, TT], F32, dma=True) for i in range(2)]
                qb = [sb(st, f"qbD{i}", [128, TT], BF16) for i in range(2)]
                t1 = [sb(st, f"t1D{i}", [128, TT], F32) for i in range(2)]
                t2 = [sb(st, f"t2D{i}", [128, TT], F32) for i in range(2)]
                P.dma("pool", [(wd.t[:], w_in.rearrange("(kc p) n -> p kc n", p=128)[:, :, O_DQ:O_DQ + 1024])], wd.ld, writes=[wd.h])
                load_x(xbs[0], tiles[0] * TT)
                for it, tix in enumerate(tiles):
                    T0 = tix * TT
                    Q0 = it * TT
                    b = it % 2
                    xb = xbs[b]
                    if it + 1 < len(tiles):
                        load_x(xbs[(it + 1) % 2], tiles[it + 1] * TT)
                    P.dma("sp", [(Ct[b].t[:], ropeD_C[:, T0:T0 + TT]), (St[b].t[:], ropeD_S[:, T0:T0 + TT])],
                          Ct[b].ld, writes=[Ct[b].h, St[b].h])
                    for h in range(8):
                        p_ = psn()
                        mm_group(P, p_, [(p_.t[:], wd.t[:, kc, h * 128:(h + 1) * 128], xb.t[:, kc, :]) for kc in range(16)],
                                 reads=[wd.h, xb.h])
                        sw = psn()
                        rope_fm(p_.t[:], p_.h, True, Ct[b], St[b], permb.t[:, 128:256], qb[h % 2], sw, t1[h % 2], t2[h % 2],
                                dqst[b].t[:, h, :], dqst[b].h)
                    P.dma("sp", [(DqT.rearrange("h p n -> p h n")[:, :, Q0:Q0 + TT], dqst[b].t[:])], dqst[b].ld,
                          reads=[dqst[b].h], accw=[hK["DqT"]])
                P.flush()

        def conv_jobs():
            jobs = []
            for kc in range(16):
                r0 = kc * 128
                for half in range(2):
                    jobs.append((w_in[r0:r0 + 128, O_G + half * 2048:O_G + (half + 1) * 2048],
                                 Wg[half * 4:(half + 1) * 4, :, kc, :].rearrange("n p j -> p n j")))
            for kc in range(8):
                r0 = kc * 128
                jobs.append((w_brm[r0:r0 + 128, :], Wbm[:, :, kc, :].rearrange("n p j -> p n j")))
                jobs.append((w_brd[r0:r0 + 128, :], Wbd[:, :, kc, :].rearrange("n p j -> p n j")))
            for kc in range(16):
                r0 = kc * 128
                jobs.append((w_out[r0:r0 + 128, :], Wo[:, :, kc, :].rearrange("n p j -> p n j")))
                for q4 in range(4):
                    jobs.append((w_ff1[r0:r0 + 128, q4 * 2048:(q4 + 1) * 2048],
                                 W1[q4 * 4:(q4 + 1) * 4, :, kc, :].rearrange("n p j -> p n j")))
            W2v = W2.rearrange("(ng kg) p kc j -> kg ng p kc j", kg=4)
            for kk in range(64):
                r0 = kk * 128
                jobs.append((w_ff2[r0:r0 + 128, :], W2v[kk // 16, :, :, kk % 16, :].rearrange("n p j -> p n j")))
            return jobs

        cjobs = conv_jobs()
        cpos = [0]

        class Conv:
            def __init__(self, st, tag):
                self.s32 = [sb(st, f"cv32{tag}{i}", [128, 2048], F32, dma=True) for i in range(2)]
                self.sbf = [sb(st, f"cvbf{tag}{i}", [128, 2048], BF16, dma=True) for i in range(2)]
                self.pend = None

            def _store(self):
                if self.pend is not None:
                    b_, dst = self.pend
                    P.dma("sp", [(dst, b_.t[:].rearrange("p (n j) -> p n j", j=dst.shape[-1]))], b_.ld,
                          reads=[b_.h], accw=[hK["W"]])
                    self.pend = None

            def steps(self, n):
                for _ in range(n):
                    if cpos[0] >= len(cjobs):
                        break
                    i = cpos[0] % 2
                    src_ap, dst = cjobs[cpos[0]]
                    cpos[0] += 1
                    a_, b_ = self.s32[i], self.sbf[i]
                    P.dma("sp", [(a_.t[:], src_ap)], a_.ld, writes=[a_.h])
                    self._store()
                    P.op("act", lambda e, a_=a_, b_=b_: e.activation(out=b_.t[:], in_=a_.t[:], func=AF.Copy),
                         reads=[a_.h], writes=[b_.h])
                    self.pend = (b_, dst)

            def finish(self):
                self._store()

        def weight_convert(st):
            stg = [sb(st, f"wcv{i}", [128, 2048], BF16, dma=True) for i in range(3)]
            stq = [P.newsem() for _ in range(3)]
            for n_, (src_ap, dst_ap) in enumerate(cjobs):
                s_ = stg[n_ % 3]
                P.dma("pool", [(s_.t[:], src_ap)], s_.ld, writes=[s_.h])
                P.dma("pool", [(dst_ap, s_.t[:].rearrange("p (n j) -> p n j", j=dst_ap.shape[-1]))], stq[n_ % 3],
                      reads=[s_.h], accw=[hK["W"]])

        def phase_E(groups, units):
            with ExitStack() as st:
                weight_convert(st)
                NR = 3
                qA = [sb(st, f"qA{i}", [128, TT], BF16, dma=True) for i in range(2)]
                qZ = [[sb(st, f"qZ{par}{i}", [128, TT], BF16, dma=True) for i in range(2)] for par in range(2)]
                for par in range(2):
                    for i in range(2):
                        P.op("dve", lambda e, z=qZ[par][i]: e.memset(z.t[:], 0.0), writes=[qZ[par][i].h])
                kA = [sb(st, f"kA{i}", [128, 1024], BF16, dma=True) for i in range(NR)]
                kB = [sb(st, f"kB{i}", [128, 1024], BF16, dma=True) for i in range(NR)]
                vv = [sb(st, f"vv{i}", [128, 8, 128], BF16, dma=True) for i in range(NR)]
                NPT = 4
                ptt = [st.enter_context(nc.sbuf_tensor(f"sb_ptt{i}", [128, 2, TT], BF16)) for i in range(NPT)]
                pt0 = [View(ptt[i][:, 0, :]) for i in range(NPT)]
                pt1 = [View(ptt[i][:, 1, :]) for i in range(NPT)]
                acc2 = [st.enter_context(nc.sbuf_tensor(f"sb_acc{i}", [128, 2, TT], F32)) for i in range(2)]
                accD = [View(acc2[i][:, 0, :]) for i in range(2)]
                accP = [View(acc2[i][:, 1, :]) for i in range(2)]
                ones32 = sb(st, "ones32", [128, 128], F32)
                rl = sb(st, "rl", [128, TT], F32)
                ob = [sb(st, f"ob{i}", [128, TT], BF16, dma=True) for i in range(2)]
                of = [sb(st, f"of{i}", [128, TT], F32, dma=True) for i in range(2)]
                P.op("dve", lambda e: e.memset(ones32.t[:], 1.0), writes=[ones32.h])
                O_ = [ps[4], ps[5]]
                L_ = [ps[6], ps[7]]
                uq = 0
                chn = 0
                pti = 0
                gpi = 0
                dfin = [None]
                dstore = [None]

                def run(slot):
                    if slot[0] is not None:
                        f_ = slot[0]
                        slot[0] = None
                        f_()

                for (q_lo, nqt, k_lo, k_len) in groups:
                    nch = k_len // 1024
                    npairs = nch * 4
                    for (kind, h, m) in units:
                        for j in range(nqt):
                            q0 = q_lo + j * TT
                            Oa, La, aD, aP = O_[uq % 2], L_[uq % 2], accD[uq % 2], accP[uq % 2]
                            if kind == "mla":
                                pb = (h % 2) * 64
                                qa_, qb_ = qA[uq % 2], qZ[h % 2][uq % 2]
                                P.dma("sp", [(qa_.t[:], QnT[h, :, q0:q0 + TT]),
                                             (qb_.t[pb:pb + 64, :], QrT[h // 2, pb:pb + 64, q0:q0 + TT])], qa_.ld,
                                      reads=[hK["QnT"], hK["QrT"]], writes=[qa_.h, qb_.h])
                                scale = 192.0 ** -0.5
                            else:
                                pb = m * 64
                                qa_ = qZ[m][uq % 2]
                                qb_ = qa_
                                P.dma("sp", [(qa_.t[pb:pb + 64, :], DqT[h, pb:pb + 64, q0:q0 + TT])], qa_.ld, reads=[hK["DqT"]],
                                      writes=[qa_.h])
                                scale = 64.0 ** -0.5
                            pending = None

                            def emit_pv(pending, Oa=Oa):
                                p0, p1, vb, ppr, first, last = pending

                                def fpv(pe):
                                    pe.matmul(Oa.t[:], vb.t[:, 2 * ppr, :], p0.t[:], start=first, stop=False)
                                    return pe.matmul(Oa.t[:], vb.t[:, 2 * ppr + 1, :], p1.t[:], start=False, stop=last)
                                P.op("pe", fpv, reads=[p0.h, p1.h, vb.h], writes=[Oa.h])

                            for c in range(nch):
                                k0 = k_lo + c * 1024
                                ci = chn % NR
                                chn += 1
                                ka_, kb_, v_ = kA[ci], kB[ci], vv[ci]
                                chx = k0 // 1024
                                if kind == "mla":
                                    P.dma("sp", [(ka_.t[:], KnT[h, :, k0:k0 + 1024]), (kb_.t[:], KrT[:, k0:k0 + 1024]),
                                                 (v_.t[:], Vm[h, chx])], ka_.ld,
                                          reads=[hK["KnT"], hK["KrT"], hK["Vm"]], writes=[ka_.h, kb_.h, v_.h])
                                else:
                                    P.dma("sp", [(ka_.t[:], DkT[h, :, k0:k0 + 1024]), (v_.t[:], Dv[h, chx])], ka_.ld,
                                          reads=[hK["DkT"], hK["Dv"]], writes=[ka_.h, v_.h])
                                if c == min(1, nch - 1):
                                    run(dstore)
                                for pr in range(4):
                                    gi = c * 4 + pr
                                    sbi = gpi % 2
                                    gpi += 1
                                    Sbig = psbig[sbi]
                                    hS = [ps[2 * sbi].h, ps[2 * sbi + 1].h]

                                    def fqk(pe, Sbig=Sbig, ka_=ka_, kb_=kb_, qa_=qa_, qb_=qb_, pr=pr, kind=kind):
                                        ins = None
                                        for t in range(2):
                                            kt = 2 * pr + t
                                            ks = slice(kt * 128, (kt + 1) * 128)
                                            o_ = Sbig[:, t * 512:(t + 1) * 512]
                                            if kind == "mla":
                                                pe.matmul(o_, ka_.t[:, ks], qa_.t[:], start=True, stop=False)
                                                ins = pe.matmul(o_, kb_.t[:, ks], qb_.t[:], start=False, stop=True)
                                            else:
                                                ins = pe.matmul(o_, ka_.t[:, ks], qa_.t[:], start=True, stop=True)
                                        return ins
                                    rd = [ka_.h, kb_.h, qa_.h, qb_.h] if kind == "mla" else [ka_.h, qa_.h]
                                    P.op("pe", fqk, reads=rd, writes=hS)
                                    if gi == 0:
                                        run(dfin)
                                    if pending is not None:
                                        emit_pv(pending)
                                    p0, p1, pfull = pt0[pti % NPT], pt1[pti % NPT], ptt[pti % NPT]
                                    pti += 1
                                    P.op("act", lambda e, pfull=pfull, Sbig=Sbig, scale=scale: e.activation(
                                        out=pfull[:].rearrange("p a n -> p (a n)"), in_=Sbig[:], func=AF.Exp, scale=scale),
                                         reads=[], writes=[p0.h, p1.h] + hS)
                                    ac_ap = acc2[uq % 2][:].rearrange("p a n -> p (a n)")
                                    pf_ap = pfull[:].rearrange("p a n -> p (a n)")
                                    if gi == 0:
                                        P.op("dve", lambda e, ac_ap=ac_ap, pf_ap=pf_ap: e.tensor_copy(out=ac_ap, in_=pf_ap),
                                             reads=[p0.h, p1.h], writes=[aD.h, aP.h])
                                    else:
                                        P.op("dve", lambda e, ac_ap=ac_ap, pf_ap=pf_ap: e.tensor_tensor(out=ac_ap, in0=ac_ap, in1=pf_ap, op=ALU.add),
                                             reads=[p0.h, p1.h, aD.h, aP.h], writes=[aD.h, aP.h])
                                    pending = (p0, p1, v_, pr, gi == 0, gi == npairs - 1)
                            emit_pv(pending)

                            def fin(La=La, Oa=Oa, aD=aD, aP=aP, kind=kind, h=h, m=m, q0=q0, uqi=uq):
                                def fl(pe):
                                    pe.matmul(La.t[:], ones32.t[:], aD.t[:], start=True, stop=False)
                                    return pe.matmul(La.t[:], ones32.t[:], aP.t[:], start=False, stop=True)
                                P.op("pe", fl, reads=[ones32.h, aD.h, aP.h], writes=[La.h])
                                P.op("dve", lambda e: e.reciprocal(out=rl.t[:], in_=La.t[:]), reads=[La.h], writes=[rl.h])
                                o_ = ob[uqi % 2] if kind == "mla" else of[uqi % 2]
                                P.op("dve", lambda e: e.tensor_tensor(out=o_.t[:], in0=Oa.t[:], in1=rl.t[:], op=ALU.mult),
                                     reads=[Oa.h, rl.h], writes=[o_.h])

                                def sto():
                                    if kind == "mla":
                                        P.dma("sp", [(OmT[h, :, q0:q0 + TT], o_.t[:])], o_.ld, reads=[o_.h], accw=[hK["OmT"]])
                                    else:
                                        P.dma("sp", [(OdT[h * 2 + m, :, q0:q0 + TT], o_.t[:])], o_.ld, reads=[o_.h], accw=[hK["OdT"]])
                                run(dstore)
                                dstore[0] = sto
                            run(dfin)
                            dfin[0] = fin
                            uq += 1
                run(dfin)
                run(dstore)
                P.flush()

        def phase_F(tiles):
            with ExitStack() as st:
                A_t = st.enter_context(nc.sbuf_tensor("sb_A", [128, 32, TT], BF16))
                A = [View(A_t[:, i, :]) for i in range(32)]
                r = [sb(st, f"r{i}", [128, TT], F32) for i in range(16)]
                x1b = [sb(st, f"x1b{i}", [128, TT], BF16) for i in range(16)]
                wp = [sb(st, f"wp{i}", [128, 16, 512], BF16, dma=True) for i in range(3)]
                xf = [sb(st, f"xf{i}", [128, TT], F32, dma=True) for i in range(3)]
                od0 = [sb(st, f"od0{i}", [128, TT], F32, dma=True) for i in range(2)]
                od1 = [sb(st, f"od1{i}", [128, TT], F32, dma=True) for i in range(2)]
                df = [sb(st, f"df{i}", [128, TT], F32) for i in range(2)]
                sqd = [sb(st, f"sqd{i}", [128, TT], BF16) for i in range(2)]
                rsd = [sb(st, f"rsd{i}", [128, TT], F32) for i in range(2)]
                gm = [sb(st, f"gm{i}", [128, TT], F32) for i in range(2)]
                gd = [sb(st, f"gd{i}", [128, TT], F32) for i in range(2)]
                ta = [sb(st, f"ta{i}", [128, TT], F32) for i in range(2)]
                tb_ = [sb(st, f"tb{i}", [128, TT], F32) for i in range(2)]
                rb = [sb(st, f"rb{i}", [128, TT], BF16) for i in range(6)]
                rq = [sb(st, f"rq{i}", [128, TT], BF16) for i in range(6)]
                meanB = sb(st, "meanB", [128, TT], F32)
                msq = sb(st, "msq", [128, TT], F32)
                rstdB = sb(st, "rstdBF", [128, TT], F32)
                yst = [P.newsem() for _ in range(2)]
                xld = P.newsem()
                omld = P.newsem()
                wpi = [0]
                xfi = [0]
                xb = A[0:16]
                om = A[16:24]
                odn = A[24:32]
                mg = x1b

                def load_panel(src_ap, dst_sl=None, w_=None):
                    if w_ is None:
                        w_ = wp[wpi[0] % 3]
                        wpi[0] += 1
                    dst = w_.t[:] if dst_sl is None else w_.t[:, dst_sl, :]
                    P.dma("sp", [(dst, src_ap)], w_.ld, reads=[hK["W"]], writes=[w_.h])
                    return w_

                def layer_norm(gcol0, bcol0, want_bf16, S1, S2):
                    P.op("dve", lambda e: e.tensor_scalar(out=meanB.t[:], in0=S1.t[:], scalar1=1.0 / D, scalar2=None, op0=ALU.mult),
                         reads=[S1.h], writes=[meanB.h])
                    P.op("dve", lambda e: e.tensor_tensor(out=msq.t[:], in0=meanB.t[:], in1=meanB.t[:], op=ALU.mult),
                         reads=[meanB.h], writes=[msq.h])
                    P.op("dve", lambda e: e.scalar_tensor_tensor(out=rstdB.t[:], in0=S2.t[:], scalar=1.0 / D, in1=msq.t[:],
                                                                 op0=ALU.mult, op1=ALU.subtract),
                         reads=[S2.h, msq.h], writes=[rstdB.h])
                    rsqrt_eps(rstdB, rstdB.t[:], rstdB.t[:], [rstdB.h], LN_EPS)
                    for oc in range(16):
                        r_ = r[oc]
                        P.op("dve", lambda e, r_=r_: e.tensor_tensor(out=r_.t[:], in0=r_.t[:], in1=meanB.t[:], op=ALU.subtract),
                             reads=[r_.h, meanB.h], writes=[r_.h])
                        P.op("dve", lambda e, r_=r_: e.tensor_tensor(out=r_.t[:], in0=r_.t[:], in1=rstdB.t[:], op=ALU.mult),
                             reads=[r_.h, rstdB.h], writes=[r_.h])
                        P.op("act", lambda e, r_=r_, oc=oc: e.activation(out=r_.t[:], in_=r_.t[:], func=AF.Identity,
                                                                          scale=vecs.t[:, gcol0 + oc:gcol0 + oc + 1],
                                                                          bias=vecs.t[:, bcol0 + oc:bcol0 + oc + 1]),
                             reads=[r_.h, vecs.h], writes=[r_.h])
                        if want_bf16:
                            P.op("act", lambda e, r_=r_, oc=oc: e.activation(out=x1b[oc].t[:], in_=r_.t[:], func=AF.Copy),
                                 reads=[r_.h], writes=[x1b[oc].h])

                sdef = []

                def flush_stats():
                    while sdef:
                        sdef.pop(0)()

                def stats(oc, S1, S2):
                    r_ = r[oc]
                    b_, q_ = rb[oc % 6], rq[oc % 6]
                    P.op("act", lambda e: e.activation(out=b_.t[:], in_=r_.t[:], func=AF.Copy), reads=[r_.h], writes=[b_.h])
                    P.op("act", lambda e: e.activation(out=q_.t[:], in_=r_.t[:], func=AF.Square), reads=[r_.h], writes=[q_.h])

                    def fst(pe):
                        pe.matmul(S1.t[:], ones.t[:], b_.t[:], start=(oc == 0), stop=(oc == 15))
                        return pe.matmul(S2.t[:], ones.t[:], q_.t[:], start=(oc == 0), stop=(oc == 15))

                    def emit():
                        P.op("pe", fst, reads=[ones.h, b_.h, q_.h], writes=[S1.h, S2.h])
                    sdef.append(emit)

                for it, tix in enumerate(tiles):
                    T0 = tix * TT
                    Q0 = it * TT
                    P.dma("pool", [(A_t[:, 0:16, :], xT_v[:, :, T0:T0 + TT])], xld, writes=[xb[kc].h for kc in range(16)])
                    P.dma("sp", [(A_t[:, 16:24, :], OmT.rearrange("h p n -> p h n")[:, :, Q0:Q0 + TT])], omld, reads=[hK["OmT"]],
                          writes=[om[h].h for h in range(8)])
                    for h in range(8):
                        i2 = h % 2
                        P.dma("sp", [(od0[i2].t[:], OdT[2 * h, :, Q0:Q0 + TT]), (od1[i2].t[:], OdT[2 * h + 1, :, Q0:Q0 + TT])],
                              od0[i2].ld, reads=[hK["OdT"]], writes=[od0[i2].h, od1[i2].h])
                        P.op("dve", lambda e, i2=i2: e.scalar_tensor_tensor(out=df[i2].t[:], in0=od1[i2].t[:], scalar=lamc.t[:, 5:6],
                                                                            in1=od0[i2].t[:], op0=ALU.mult, op1=ALU.add),
                             reads=[od0[i2].h, od1[i2].h, lamc.h], writes=[df[i2].h])
                        P.op("act", lambda e, i2=i2: e.activation(out=sqd[i2].t[:], in_=df[i2].t[:], func=AF.Square),
                             reads=[df[i2].h], writes=[sqd[i2].h])
                        p_ = psn()
                        mm_group(P, p_, [(p_.t[:], ones.t[:], sqd[i2].t[:])], reads=[ones.h, sqd[i2].h])
                        rsqrt_eps(rsd[i2], rsd[i2].t[:], p_.t[:], [p_.h], 128.0 * RMS_EPS)
                        P.op("dve", lambda e, i2=i2, h=h: e.scalar_tensor_tensor(out=odn[h].t[:], in0=df[i2].t[:], scalar=lamc.t[:, 6:7],
                                                                                 in1=rsd[i2].t[:], op0=ALU.mult, op1=ALU.mult),
                             reads=[df[i2].h, rsd[i2].h, lamc.h], writes=[odn[h].h])
                    for ocg in range(4):
                        wgm = load_panel(Wg[ocg])
                        wgd = load_panel(Wg[4 + ocg])
                        c0 = (ocg % 2) * 512
                        wbr = load_panel(Wbm[ocg // 2, :, :, c0:c0 + 512], slice(0, 8))
                        load_panel(Wbd[ocg // 2, :, :, c0:c0 + 512], slice(8, 16), w_=wbr)
                        for o4 in range(4):
                            oc = ocg * 4 + o4
                            pm, pd, pgm, pgd = psn(), psn(), psn(), psn()
                            cs = slice(o4 * 128, (o4 + 1) * 128)
                            mm_group(P, pgm, [(pgm.t[:], wgm.t[:, kc, cs], xb[kc].t[:]) for kc in range(16)],
                                     reads=[wgm.h] + [xb[kc].h for kc in range(16)])
                            mm_group(P, pgd, [(pgd.t[:], wgd.t[:, kc, cs], xb[kc].t[:]) for kc in range(16)],
                                     reads=[wgd.h] + [xb[kc].h for kc in range(16)])
                            mm_group(P, pm, [(pm.t[:], wbr.t[:, k, cs], om[k].t[:]) for k in range(8)],
                                     reads=[wbr.h] + [om[k].h for k in range(8)])
                            mm_group(P, pd, [(pd.t[:], wbr.t[:, 8 + k, cs], odn[k].t[:]) for k in range(8)],
                                     reads=[wbr.h] + [odn[k].h for k in range(8)])
                            i2 = oc % 2
                            P.op("act", lambda e, pgm=pgm, i2=i2, oc=oc: e.activation(out=gm[i2].t[:], in_=pgm.t[:], func=AF.Sigmoid,
                                                                                      bias=vecs.t[:, V_BG + oc:V_BG + oc + 1]),
                                 reads=[pgm.h, vecs.h], writes=[gm[i2].h])
                            P.op("act", lambda e, pgd=pgd, i2=i2, oc=oc: e.activation(out=gd[i2].t[:], in_=pgd.t[:], func=AF.Sigmoid,
                                                                                      bias=vecs.t[:, V_BG + 16 + oc:V_BG + 16 + oc + 1]),
                                 reads=[pgd.h, vecs.h], writes=[gd[i2].h])
                            P.op("dve", lambda e, pm=pm, i2=i2: e.tensor_tensor(out=ta[i2].t[:], in0=pm.t[:], in1=gm[i2].t[:], op=ALU.mult),
                                 reads=[pm.h, gm[i2].h], writes=[ta[i2].h])
                            P.op("dve", lambda e, pd=pd, i2=i2: e.tensor_tensor(out=tb_[i2].t[:], in0=pd.t[:], in1=gd[i2].t[:], op=ALU.mult),
                                 reads=[pd.h, gd[i2].h], writes=[tb_[i2].h])
                            P.op("dve", lambda e, i2=i2, oc=oc: e.tensor_tensor(out=mg[oc].t[:], in0=ta[i2].t[:], in1=tb_[i2].t[:], op=ALU.add),
                                 reads=[ta[i2].h, tb_[i2].h], writes=[mg[oc].h])
                    S1, S2 = ps[6], ps[7]
                    for ocg in range(4):
                        wo_ = load_panel(Wo[ocg])
                        for o4 in range(4):
                            oc = ocg * 4 + o4
                            acc = ps[(oc % 4)]
                            mm_group(P, acc, [(acc.t[:], wo_.t[:, kc, o4 * 128:(o4 + 1) * 128], mg[kc].t[:]) for kc in range(16)],
                                     reads=[wo_.h] + [mg[kc].h for kc in range(16)])
                            flush_stats()
                            x_ = xf[xfi[0] % 3]
                            xfi[0] += 1
                            P.dma("sp", [(x_.t[:], xT[oc * 128:(oc + 1) * 128, T0:T0 + TT])], x_.ld, writes=[x_.h])
                            r_ = r[oc]
                            P.op("dve", lambda e, r_=r_, x_=x_, acc=acc: e.scalar_tensor_tensor(
                                out=r_.t[:], in0=x_.t[:], scalar=ALPHA, in1=acc.t[:], op0=ALU.mult, op1=ALU.add),
                                 reads=[x_.h, acc.h], writes=[r_.h])
                            stats(oc, S1, S2)
                    flush_stats()
                    layer_norm(V_L1G, V_L1B, True, S1, S2)
                    for hf in range(2):
                        for pn in range(hf * 8, hf * 8 + 8):
                            w1_ = load_panel(W1[pn])
                            for o4 in range(4):
                                fc = pn * 4 + o4
                                fl = fc - hf * 32
                                acc = ps[4 + fc % 4] if hf == 1 else ps[fc % 6]
                                mm_group(P, acc, [(acc.t[:], w1_.t[:, kc, o4 * 128:(o4 + 1) * 128], x1b[kc].t[:]) for kc in range(16)],
                                         reads=[w1_.h] + [x1b[kc].h for kc in range(16)])
                                i2 = fc % 2
                                P.op("act", lambda e, acc=acc, i2=i2: e.activation(out=ta[i2].t[:], in_=acc.t[:], func=AF.Relu),
                                     reads=[acc.h], writes=[ta[i2].h])
                                P.op("dve", lambda e, i2=i2, fl=fl: e.tensor_tensor(out=A[fl].t[:], in0=ta[i2].t[:], in1=ta[i2].t[:], op=ALU.mult),
                                     reads=[ta[i2].h], writes=[A[fl].h])
                        for ng in range(4):
                            accs = [ps[0], ps[1], ps[2], ps[3]]
                            for k2 in range(2):
                                kg = hf * 2 + k2
                                w2_ = load_panel(W2[ng * 4 + kg])

                                def f2(pe, w2_=w2_, k2=k2, accs=accs):
                                    ins = None
                                    for o4 in range(4):
                                        for kc in range(16):
                                            ins = pe.matmul(accs[o4].t[:], w2_.t[:, kc, o4 * 128:(o4 + 1) * 128], A[k2 * 16 + kc].t[:],
                                                            start=(k2 == 0 and kc == 0), stop=(k2 == 1 and kc == 15))
                                    return ins
                                P.op("pe", f2, reads=[w2_.h] + [A[k2 * 16 + kc].h for kc in range(16)], writes=[a.h for a in accs])
                            flush_stats()
                            for o4 in range(4):
                                oc = ng * 4 + o4
                                r_, acc = r[oc], accs[o4]
                                if hf == 0:
                                    P.op("dve", lambda e, r_=r_, acc=acc: e.scalar_tensor_tensor(
                                        out=r_.t[:], in0=r_.t[:], scalar=ALPHA, in1=acc.t[:], op0=ALU.mult, op1=ALU.add),
                                         reads=[r_.h, acc.h], writes=[r_.h])
                                else:
                                    P.op("dve", lambda e, r_=r_, acc=acc: e.tensor_tensor(out=r_.t[:], in0=acc.t[:], in1=r_.t[:], op=ALU.add),
                                         reads=[r_.h, acc.h], writes=[r_.h])
                                    stats(oc, S1, S2)
                    flush_stats()
                    layer_norm(V_L2G, V_L2B, False, S1, S2)
                    ys = yst[it % 2]
                    P.dma("sp", [(yT[oc * 128:(oc + 1) * 128, Q0:Q0 + TT], r[oc].t[:]) for oc in range(16)], ys,
                          reads=[r[oc].h for oc in range(16)], accw=[hK["yT"]])
                P.op("sp", None, reads=[hK["yT"]])
                P.flush()

        groups = [(0, 4, 0, SP_LEN), (NQ // 2, 4, SP_LEN, SS_LEN)]
        units = [("mla", h, 0) for h in range(8)] + [("diff", h, m) for h in range(8) for m in range(2)]
        own = OWN_TILES if own is None else own
        if "A" in phases:
            phase_A(list(range(ntk)))
        if "B" in phases:
            phase_B(list(range(ntk)))
        if "C" in phases:
            phase_C(own)
        if "D" in phases:
            phase_D(own)
        if "E" in phases:
            phase_E(groups if egroups is None else egroups, units if eunits is None else eunits)
        if "F" in phases:
            phase_F(own)
        if "F" not in phases:
            P.flush()
    return nc


def _rope_tables(pos):
    pos = pos.astype(np.float32)
    inv_m = (np.float32(ROPE_THETA) ** (-np.arange(0, 64, 2, dtype=np.float32) / np.float32(64))).astype(np.float32)
    ang_m = (pos[None, :] * inv_m[:, None]).astype(np.float32)
    cm, sm = np.cos(ang_m).astype(np.float32), np.sin(ang_m).astype(np.float32)
    C64 = np.concatenate([cm, cm], 0)
    S64 = np.concatenate([-sm, sm], 0)
    MC = np.concatenate([C64, C64], 0)
    MS = np.concatenate([S64, S64], 0)
    inv_d = (np.float32(ROPE_THETA) ** (-np.arange(0, 16, 2, dtype=np.float32) / np.float32(16))).astype(np.float32)
    ang_d = (pos[None, :] * inv_d[:, None]).astype(np.float32)
    cd, sd = np.cos(ang_d).astype(np.float32), np.sin(ang_d).astype(np.float32)
    n = pos.shape[0]
    C1 = np.concatenate([cd, cd, np.ones((48, n), np.float32)], 0)
    S1 = np.concatenate([-sd, sd, np.zeros((48, n), np.float32)], 0)
    DC = np.concatenate([C1, C1], 0)
    DS = np.concatenate([S1, S1], 0)
    return [np.ascontiguousarray(a) for a in (MC, MS, DC, DS)]


def _perms():
    pm = np.zeros((128, 256), np.float32)
    for i in range(128):
        blk, d = (i // 64) * 64, i % 64
        pm[blk + (d + 32) % 64, i] = 1.0
        if d < 16:
            pm[blk + (d + 8) % 16, 128 + i] = 1.0
        else:
            pm[i, 128 + i] = 1.0
    return pm


_NC_CACHE = {}


def kernel(x_prompt, x_sample, w_in, b_gate, g_qa, w_qb, g_kva, w_kvb, lam_q, lam_k, g_sub,
           w_br_mla, w_br_diff, w_out, ln1_g, ln1_b, w_ff1, w_ff2, ln2_g, ln2_b, _debug=False, _bkw=None):
    f = lambda a: np.ascontiguousarray(np.asarray(a, dtype=np.float32))
    xp = f(x_prompt)[0]
    xs = f(x_sample)
    col = lambda v, n: f(v).reshape(n, 128).T
    vecs = np.zeros((128, NV), np.float32)
    vecs[:, V_BG:V_BG + 32] = col(b_gate, 32)
    vecs[:, V_GQA:V_GQA + 6] = col(g_qa, 6)
    vecs[:, V_GKVA:V_GKVA + 4] = col(g_kva, 4)
    vecs[:, V_L1G:V_L1G + 16] = col(ln1_g, 16)
    vecs[:, V_L1B:V_L1B + 16] = col(ln1_b, 16)
    vecs[:, V_L2G:V_L2G + 16] = col(ln2_g, 16)
    vecs[:, V_L2B:V_L2B + 16] = col(ln2_b, 16)
    vecs[:, V_GSUB] = f(g_sub).reshape(128)
    lamqk = np.ascontiguousarray(np.broadcast_to(
        np.concatenate([f(lam_q).reshape(128), f(lam_k).reshape(128)])[None, :], (128, 256)))
    perms = _perms()
    shared = {
        "w_in": f(w_in)[0], "w_qb": f(w_qb)[0], "w_kvb": f(w_kvb)[0], "w_br_mla": f(w_br_mla)[0],
        "w_br_diff": f(w_br_diff)[0], "w_out": f(w_out)[0], "w_ff1": f(w_ff1)[0], "w_ff2": f(w_ff2)[0],
        "vecs": vecs, "lamqk": lamqk, "perms": perms,
    }
    in_maps = []
    for c in range(8):
        own_p = np.arange(2048 * c, 2048 * (c + 1))
        rest_p = np.concatenate([np.arange(0, 2048 * c), np.arange(2048 * (c + 1), SP_LEN)])
        sb_, hf = c // 2, c % 2
        own_s = np.arange(2048 * hf, 2048 * (hf + 1))
        rest_s = np.arange(2048 * (1 - hf), 2048 * (2 - hf))
        pos = np.concatenate([own_p, rest_p, own_s, rest_s])
        xk = np.concatenate([xp[own_p], xp[rest_p], xs[sb_][own_s], xs[sb_][rest_s]], 0)
        MC, MS, DC, DS = _rope_tables(pos)
        m = dict(shared)
        m.update({"xT": np.ascontiguousarray(xk.T), "ropeM_C": MC, "ropeM_S": MS, "ropeD_C": DC, "ropeD_S": DS})
        in_maps.append(m)
    key = bool(_debug)
    if key not in _NC_CACHE:
        _NC_CACHE[key] = build_program(debug=_debug, **(_bkw or {}))
    nc = _NC_CACHE[key]
    if _debug:
        return run_bass_kernel_spmd(nc, in_maps[:1], core_ids=[0]), in_maps[0]
    res = run_bass_kernel_spmd(nc, in_maps, core_ids=list(range(8)))
    y_prompt = np.empty((1, SP_LEN, D), np.float32)
    y_sample = np.empty((4, SS_LEN, D), np.float32)
    for c in range(8):
        yT = np.asarray(res.results[c]["yT"], dtype=np.float32)
        y_prompt[0, 2048 * c:2048 * (c + 1), :] = yT[:, 0:2048].T
        y_sample[c // 2, 2048 * (c % 2):2048 * (c % 2 + 1), :] = yT[:, 2048:4096].T
    return (y_prompt, y_sample)
```

```python
import math
import os
from contextlib import ExitStack

import numpy as np
import concourse.bass as bass
import concourse.mybir as mybir
from concourse.bass_utils import run_bass_kernel_spmd

F32 = mybir.dt.float32
BF16 = mybir.dt.bfloat16
AF = mybir.ActivationFunctionType
ALU = mybir.AluOpType
AX = mybir.AxisListType

D = 2048
KC = 16
SP_LEN = 16384
SS_LEN = 4096
NK = SP_LEN + SS_LEN
NQ = 4096
TT = 512
NT_K = NK // TT
OWN_TILES = [0, 1, 2, 3, 32, 33, 34, 35]
C_QA, C_KVA, C_DQ, C_DK, C_DV = 768, 576, 1024, 1024, 1024
O_KVA = C_QA
O_DQ = C_QA + C_KVA
O_DK = O_DQ + C_DQ
O_DV = O_DK + C_DK
O_G = O_DV + C_DV
LN_EPS = 1e-5
RMS_EPS = 1e-6
ALPHA = 2.0 ** 0.25
LAM_INIT = 0.8 - 0.6 * math.exp(0.0)
ROPE_THETA = 500000.0
V_BG, V_GQA, V_GKVA, V_L1G, V_L1B, V_L2G, V_L2B, V_GSUB, NV = 0, 32, 38, 42, 58, 74, 90, 106, 107


class Sem:
    def __init__(self, h):
        self.h = h
        self.count = 0


class H:
    __slots__ = ("w", "r")

    def __init__(self):
        self.w = {}
        self.r = {}


class Eng:
    def __init__(self, name, sem, selfwait):
        self.name = name
        self.sem = sem
        self.ops = []
        self.waited = {}
        self.selfwait = selfwait


class Prog:
    def __init__(self, nc, es):
        self.nc = nc
        mk = lambda n: Sem(es.enter_context(nc.semaphore(n)))
        self.E = {
            "pe": Eng("pe", mk("s_pe"), False),
            "act": Eng("act", mk("s_act"), True),
            "dve": Eng("dve", mk("s_dve"), True),
            "pool": Eng("pool", mk("s_pool"), True),
            "sp": Eng("sp", mk("s_sp"), False),
        }
        self.dsems = [mk(f"s_d{i}") for i in range(72)]
        self.swsems = [mk(f"s_w{i}") for i in range(8)]
        self.dptr = 0
        self.swptr = 0
        self.nops = 0

    def newsem(self, sw=False):
        if sw:
            s = self.swsems[self.swptr % len(self.swsems)]
            self.swptr += 1
            return s
        s = self.dsems[self.dptr % len(self.dsems)]
        self.dptr += 1
        return s

    def op(self, eng, fn, reads=(), writes=(), accw=(), dsem=None, ndma=1):
        E = self.E[eng]
        need = {}

        def add(s, v):
            if v > need.get(s, 0):
                need[s] = v

        for h in reads:
            for s, v in h.w.items():
                add(s, v)
        for h in writes:
            for s, v in h.w.items():
                if s is not E.sem:
                    add(s, v)
            for s, v in h.r.items():
                if s is not E.sem:
                    add(s, v)
        waits = []
        for s, v in need.items():
            if s is E.sem and not E.selfwait:
                continue
            if E.waited.get(s, 0) >= v:
                continue
            E.waited[s] = v
            waits.append((s, v))
        if fn is None:
            E.ops.append((waits, None, None, 0))
            return
        if dsem is not None:
            sem = dsem
            sem.count += 16 * ndma
        else:
            sem = E.sem
            sem.count += 1
        val = sem.count
        E.ops.append((waits, fn, sem, 16 if dsem is not None else 1))
        self.nops += 1
        for h in reads:
            if h.r.get(sem, 0) < val:
                h.r[sem] = val
        for h in writes:
            h.w = {sem: val}
            h.r = {}
        for h in accw:
            h.w[sem] = val

    def dma(self, q, pairs, dsem, reads=(), writes=(), accw=(), **kw):
        def fn(eng, pairs=pairs, kw=kw):
            return [eng.dma_start(out=o, in_=i, **kw) for o, i in pairs]
        self.op(q, fn, reads=reads, writes=writes, accw=accw, dsem=dsem, ndma=len(pairs))

    def flush(self):
        nc = self.nc
        sp = self.E["sp"]
        fw = []
        for s in self.dsems + self.swsems:
            if s.count > 0 and sp.waited.get(s, 0) < s.count:
                sp.waited[s] = s.count
                fw.append((s, s.count))
        sp.ops.append((fw, None, None, 0))
        with nc.Block() as block:
            for name, deco in (("pe", block.tensor), ("act", block.scalar), ("dve", block.vector),
                               ("pool", block.gpsimd), ("sp", block.sync)):
                ops = self.E[name].ops

                def body(eng, ops=ops):
                    for waits, fn, sem, inc in ops:
                        for s, v in waits:
                            eng.wait_ge(s.h, v)
                        if fn is None:
                            continue
                        ins = fn(eng)
                        if isinstance(ins, (list, tuple)):
                            for i_ in ins:
                                i_.then_inc(sem.h, inc)
                        else:
                            ins.then_inc(sem.h, inc)

                deco(body)
        for e in self.E.values():
            e.ops = []
        self.dptr = 0
        self.swptr = 0


class Buf:
    def __init__(self, t, P=None, dma=False):
        self.t = t
        self.h = H()
        self.ld = P.newsem(sw=(dma == "sw")) if dma else None


class _T:
    def __init__(self, ap):
        self.ap = ap

    def __getitem__(self, idx):
        return self.ap[idx]


class View:
    def __init__(self, ap):
        self.t = _T(ap)
        self.h = H()


def mm_group(P, out_buf, pieces, reads, extra_writes=()):
    def fn(pe, pieces=pieces):
        n = len(pieces)
        ins = None
        for i, (o, l, r) in enumerate(pieces):
            ins = pe.matmul(o, l, r, start=(i == 0), stop=(i == n - 1))
        return ins
    P.op("pe", fn, reads=reads, writes=[out_buf.h] + list(extra_writes))


def build_program(debug=False, phases="ABCDEF", ntk=NT_K, own=None, egroups=None, eunits=None):
    nc = bass.Bass("TRN2", target_bir_lowering=False)
    es = ExitStack()
    with es:
        def din(name, shape, dt=F32):
            return nc.dram_tensor(name, shape, dt, kind="ExternalInput").ap()

        def dscr(name, shape, dt):
            return nc.dram_tensor(name, shape, dt, kind="ExternalOutput" if debug else "Internal").ap()

        xT = din("xT", [D, NK])
        w_in = din("w_in", [D, 8512])
        w_qb = din("w_qb", [768, 1536])
        w_kvb = din("w_kvb", [512, 2048])
        w_brm = din("w_br_mla", [1024, 2048])
        w_brd = din("w_br_diff", [1024, 2048])
        w_out = din("w_out", [D, D])
        w_ff1 = din("w_ff1", [D, 8192])
        w_ff2 = din("w_ff2", [8192, D])
        vecs_d = din("vecs", [128, NV])
        lamqk_d = din("lamqk", [128, 256])
        ropeM_C = din("ropeM_C", [128, NK])
        ropeM_S = din("ropeM_S", [128, NK])
        ropeD_C = din("ropeD_C", [128, NK])
        ropeD_S = din("ropeD_S", [128, NK])
        perm_d = din("perms", [128, 256])
        yT = nc.dram_tensor("yT", [D, NQ], F32, kind="ExternalOutput").ap()

        NCH = NK // 1024
        KnT = dscr("KnT", [8, 128, NK], BF16)
        KrT = dscr("KrT", [128, NK], BF16)
        Vm = dscr("Vm", [8, NCH, 128, 8, 128], BF16)
        DkT = dscr("DkT", [8, 128, NK], BF16)
        Dv = dscr("Dv", [8, NCH, 128, 8, 128], BF16)
        QnT = dscr("QnT", [8, 128, NQ], BF16)
        QrT = dscr("QrT", [4, 128, NQ], BF16)
        DqT = dscr("DqT", [8, 128, NQ], BF16)
        OmT = dscr("OmT", [8, 128, NQ], BF16)
        OdT = dscr("OdT", [16, 128, NQ], F32)
        Wg = dscr("Wg", [8, 128, 16, 512], BF16)
        Wbm = dscr("Wbm", [2, 128, 8, 1024], BF16)
        Wbd = dscr("Wbd", [2, 128, 8, 1024], BF16)
        Wo = dscr("Wo", [4, 128, 16, 512], BF16)
        W1 = dscr("W1", [16, 128, 16, 512], BF16)
        W2 = dscr("W2", [16, 128, 16, 512], BF16)
        hK = {n: H() for n in ("KnT", "KrT", "Vm", "DkT", "Dv", "QnT", "QrT", "DqT", "OmT", "OdT", "W", "yT")}

        P = Prog(nc, es)

        def sb(stack, name, shape, dt, dma=False):
            return Buf(stack.enter_context(nc.sbuf_tensor("sb_" + name, shape, dt)), P, dma)

        psbig = [es.enter_context(nc.psum_tensor(f"pb{i}", [128, 1024], F32)) for i in range(4)]
        ps = [View(psbig[i // 2][:, (i % 2) * 512:(i % 2 + 1) * 512]) for i in range(8)]
        vecs = sb(es, "vecs", [128, NV], F32, dma=True)
        lamqk = sb(es, "lamqk", [128, 256], F32, dma=True)
        perm32 = sb(es, "perm32", [128, 256], F32, dma=True)
        permb = sb(es, "permb", [128, 256], BF16)
        ones = sb(es, "ones", [128, 128], BF16)
        lamt = sb(es, "lamt", [128, 128], F32)
        lamc = sb(es, "lamc", [128, 8], F32)
        gq = sb(es, "gq", [128, 6], F32)
        gk = sb(es, "gk", [128, 4], F32)
        epsb = sb(es, "epsb", [128, 4], F32)
        EPS_LIST = [512.0 * RMS_EPS, 768.0 * RMS_EPS, 128.0 * RMS_EPS, LN_EPS]
        epsc = {v: epsb.t[:, i:i + 1] for i, v in enumerate(EPS_LIST)}

        P.dma("sp", [(vecs.t[:], vecs_d[:, :])], vecs.ld, writes=[vecs.h])
        P.dma("sp", [(lamqk.t[:], lamqk_d[:, :])], lamqk.ld, writes=[lamqk.h])
        P.dma("sp", [(perm32.t[:], perm_d[:, :])], perm32.ld, writes=[perm32.h])
        P.op("dve", lambda e: e.tensor_copy(out=permb.t[:], in_=perm32.t[:]), reads=[perm32.h], writes=[permb.h])
        P.op("dve", lambda e: e.memset(ones.t[:], 1.0), writes=[ones.h])
        for i_, v_ in enumerate(EPS_LIST):
            P.op("dve", lambda e, i_=i_, v_=v_: e.memset(epsb.t[:, i_:i_ + 1], v_), writes=[epsb.h])
        P.op("dve", lambda e: e.tensor_tensor(out=lamt.t[:], in0=lamqk.t[:, 0:128], in1=lamqk.t[:, 128:256], op=ALU.mult),
             reads=[lamqk.h], writes=[lamt.h])
        P.op("dve", lambda e: e.tensor_reduce(out=lamc.t[:, 0:1], in_=lamt.t[:, 0:64], axis=AX.X, op=ALU.add),
             reads=[lamt.h], writes=[lamc.h])
        P.op("dve", lambda e: e.tensor_reduce(out=lamc.t[:, 1:2], in_=lamt.t[:, 64:128], axis=AX.X, op=ALU.add),
             reads=[lamt.h, lamc.h], writes=[lamc.h])
        P.op("act", lambda e: e.activation(out=lamc.t[:, 2:4], in_=lamc.t[:, 0:2], func=AF.Exp),
             reads=[lamc.h], writes=[lamc.h])
        P.op("dve", lambda e: e.tensor_tensor(out=lamc.t[:, 7:8], in0=lamc.t[:, 2:3], in1=lamc.t[:, 3:4], op=ALU.subtract),
             reads=[lamc.h], writes=[lamc.h])
        P.op("dve", lambda e: e.tensor_scalar(out=lamc.t[:, 5:6], in0=lamc.t[:, 7:8], scalar1=LAM_INIT, scalar2=-1.0,
                                              op0=ALU.add, op1=ALU.mult), reads=[lamc.h], writes=[lamc.h])
        P.op("dve", lambda e: e.tensor_scalar(out=lamc.t[:, 6:7], in0=vecs.t[:, V_GSUB:V_GSUB + 1],
                                              scalar1=(1.0 - LAM_INIT) * math.sqrt(128.0), scalar2=None, op0=ALU.mult),
             reads=[vecs.h, lamc.h], writes=[lamc.h])
        P.op("dve", lambda e: e.tensor_scalar(out=gq.t[:], in0=vecs.t[:, V_GQA:V_GQA + 6], scalar1=math.sqrt(768.0),
                                              scalar2=None, op0=ALU.mult), reads=[vecs.h], writes=[gq.h])
        P.op("dve", lambda e: e.tensor_scalar(out=gk.t[:], in0=vecs.t[:, V_GKVA:V_GKVA + 4], scalar1=math.sqrt(512.0),
                                              scalar2=None, op0=ALU.mult), reads=[vecs.h], writes=[gk.h])

        xT_v = xT.rearrange("(kc p) n -> p kc n", p=128)

        def load_x(xb, T0):
            P.dma("pool", [(xb.t[:], xT_v[:, :, T0:T0 + TT])], xb.ld, writes=[xb.h])

        def rope_fm(src_ap, src_h, src_is_psum, Ct, St, perm_ap, qb, sw, t1, t2, out_ap, out_h):
            xw = [src_h] if src_is_psum else []
            P.op("act", lambda e: e.activation(out=qb.t[:], in_=src_ap, func=AF.Copy), reads=[src_h], writes=[qb.h] + xw)
            mm_group(P, sw, [(sw.t[:], perm_ap, qb.t[:])], reads=[qb.h, permb.h])
            P.op("dve", lambda e: e.tensor_tensor(out=t1.t[:], in0=src_ap, in1=Ct.t[:], op=ALU.mult),
                 reads=[src_h, Ct.h], writes=[t1.h] + xw)
            P.op("dve", lambda e: e.tensor_tensor(out=t2.t[:], in0=sw.t[:], in1=St.t[:], op=ALU.mult),
                 reads=[sw.h, St.h], writes=[t2.h])
            P.op("dve", lambda e: e.tensor_tensor(out=out_ap, in0=t1.t[:], in1=t2.t[:], op=ALU.add),
                 reads=[t1.h, t2.h], writes=[out_h])

        def rsqrt_eps(out_buf, out_ap, in_ap, in_hs, eps):
            P.op("act", lambda e: e.activation(out=out_ap, in_=in_ap, func=AF.Ln, bias=epsc[eps][:, 0:1]), reads=list(in_hs) + [epsb.h],
                 writes=[out_buf.h])
            P.op("act", lambda e: e.activation(out=out_ap, in_=out_ap, func=AF.Exp, scale=-0.5), reads=[out_buf.h], writes=[out_buf.h])

        rr = [0]

        def psn():
            rr[0] += 1
            return ps[rr[0] % 8]

        def phase_A(tiles):
            with ExitStack() as st:
                wA = sb(st, "wA", [128, 16, 640], BF16, dma="sw")
                wkvb = sb(st, "wkvb", [128, 4, 2048], BF16)
                wst = [sb(st, f"wstA{i}", [128, 2048], F32, dma=True) for i in range(2)]
                xbs = [sb(st, f"xbA{i}", [128, 16, TT], BF16, dma="sw") for i in range(2)]
                ckvb = [sb(st, f"ckvb{i}", [128, 4, TT], BF16) for i in range(2)]
                sq = [sb(st, f"sqA{i}", [128, 4, TT], BF16) for i in range(2)]
                rstdB = [sb(st, f"rstdBA{i}", [128, TT], F32) for i in range(2)]
                rcol = [sb(st, f"rcolA{i}", [128, 4], F32) for i in range(2)]
                knst = [sb(st, f"knst{i}", [128, 8, TT], BF16, dma=True) for i in range(2)]
                vst = [sb(st, f"vst{i}", [128, 8, 4, 128], BF16, dma=True) for i in range(2)]
                krst = [sb(st, f"krst{i}", [128, TT], BF16, dma=True) for i in range(2)]
                Ct = [sb(st, f"CtA{i}", [128, TT], F32, dma=True) for i in range(2)]
                St = [sb(st, f"StA{i}", [128, TT], F32, dma=True) for i in range(2)]
                qb = sb(st, "qbA", [128, TT], BF16)
                t1 = sb(st, "t1A", [128, TT], F32)
                t2 = sb(st, "t2A", [128, TT], F32)
                wA2 = P.newsem(sw=True)
                P.dma("pool", [(wA.t[:, :, 0:576], w_in.rearrange("(kc p) n -> p kc n", p=128)[:, :, O_KVA:O_KVA + 576])], wA.ld, accw=[wA.h])
                P.dma("pool", [(wA.t[:, :, 576:640], w_in.rearrange("(kc p) n -> p kc n", p=128)[:, :, O_KVA + 512:O_KVA + 576])], wA2, accw=[wA.h])
                wkvb_v = w_kvb.rearrange("(kc p) (h t d) -> kc t p h d", p=128, t=2, d=128)
                for kc in range(4):
                    s_ = wst[kc % 2]
                    P.dma("sp", [(s_.t[:, 0:1024].rearrange("p (h d) -> p h d", d=128), wkvb_v[kc, 0]),
                                 (s_.t[:, 1024:2048].rearrange("p (h d) -> p h d", d=128), wkvb_v[kc, 1])],
                          s_.ld, writes=[s_.h])
                    P.op("act", lambda e, s_=s_, kc=kc: e.activation(out=wkvb.t[:, kc, :], in_=s_.t[:], func=AF.Identity,
                                                                      scale=gk.t[:, kc:kc + 1]),
                         reads=[s_.h, gk.h], writes=[wkvb.h])
                CUT = int(os.environ.get("CUTA", "99"))
                if CUT <= 0:
                    P.flush()
                    return
                load_x(xbs[0], tiles[0] * TT)
                for it, tix in enumerate(tiles):
                    T0 = tix * TT
                    b = it % 2
                    xb = xbs[b]
                    if it + 1 < len(tiles):
                        load_x(xbs[(it + 1) % 2], tiles[it + 1] * TT)
                    P.dma("sp", [(Ct[b].t[:], ropeM_C[:, T0:T0 + TT]), (St[b].t[:], ropeM_S[:, T0:T0 + TT])],
                          Ct[b].ld, writes=[Ct[b].h, St[b].h])
                    if CUT <= 1:
                        continue
                    for oc in range(4):
                        p_ = psn()
                        mm_group(P, p_, [(p_.t[:], wA.t[:, kc, oc * 128:(oc + 1) * 128], xb.t[:, kc, :]) for kc in range(16)],
                                 reads=[wA.h, xb.h])
                        P.op("act", lambda e, p_=p_, oc=oc, b=b: e.activation(out=ckvb[b].t[:, oc, :], in_=p_.t[:], func=AF.Copy),
                             reads=[p_.h], writes=[ckvb[b].h])
                        P.op("act", lambda e, p_=p_, oc=oc, b=b: e.activation(out=sq[b].t[:, oc, :], in_=p_.t[:], func=AF.Square),
                             reads=[p_.h], writes=[sq[b].h])
                    if CUT <= 2:
                        continue
                    p_ = psn()
                    mm_group(P, p_, [(p_.t[:], ones.t[:], sq[b].t[:, oc, :]) for oc in range(4)], reads=[ones.h, sq[b].h])
                    rsqrt_eps(rstdB[b], rstdB[b].t[:], p_.t[:], [p_.h], 512.0 * RMS_EPS)
                    p_ = psn()

                    def fcol(pe, p_=p_, b=b):
                        ins = None
                        for tb in range(4):
                            for oc in range(4):
                                ins = pe.matmul(p_.t[:, tb:tb + 1], sq[b].t[:, oc, tb * 128:(tb + 1) * 128], ones.t[:, 0:1],
                                                start=(oc == 0), stop=(oc == 3))
                        return ins
                    P.op("pe", fcol, reads=[ones.h, sq[b].h], writes=[p_.h])
                    rsqrt_eps(rcol[b], rcol[b].t[:], p_.t[:, 0:4], [p_.h], 512.0 * RMS_EPS)
                    if CUT <= 3:
                        continue
                    for h in range(8):
                        p_ = psn()
                        mm_group(P, p_, [(p_.t[:], wkvb.t[:, c4, h * 128:(h + 1) * 128], ckvb[b].t[:, c4, :]) for c4 in range(4)],
                                 reads=[wkvb.h, ckvb[b].h])
                        P.op("dve", lambda e, p_=p_, h=h, b=b: e.tensor_tensor(out=knst[b].t[:, h, :], in0=p_.t[:], in1=rstdB[b].t[:],
                                                                               op=ALU.mult),
                             reads=[p_.h, rstdB[b].h], writes=[knst[b].h])
                    P.dma("sp", [(KnT.rearrange("h p n -> p h n")[:, :, T0:T0 + TT], knst[b].t[:])], knst[b].ld,
                          reads=[knst[b].h], accw=[hK["KnT"]])
                    if CUT <= 4:
                        continue
                    for tb in range(4):
                        for half in range(2):
                            p_ = psn()
                            mm_group(P, p_, [(p_.t[:], ckvb[b].t[:, c4, tb * 128:(tb + 1) * 128],
                                              wkvb.t[:, c4, 1024 + half * 512:1024 + (half + 1) * 512]) for c4 in range(4)],
                                     reads=[wkvb.h, ckvb[b].h])
                            P.op("act", lambda e, p_=p_, tb=tb, half=half, b=b: e.activation(
                                out=vst[b].t[:, half * 4:(half + 1) * 4, tb, :],
                                in_=p_.t[:].rearrange("p (h d) -> p h d", d=128), func=AF.Identity, scale=rcol[b].t[:, tb:tb + 1]),
                                 reads=[p_.h, rcol[b].h], writes=[vst[b].h])
                    ch, k0 = T0 // 1024, (T0 % 1024) // 128
                    P.dma("sp", [(Vm[:, ch, :, k0:k0 + 4, :].rearrange("h p k d -> p h k d"), vst[b].t[:])], vst[b].ld,
                          reads=[vst[b].h], accw=[hK["Vm"]])
                    if CUT <= 5:
                        continue
                    p_ = psn()
                    mm_group(P, p_, [(p_.t[:], wA.t[:, kc, 512:640], xb.t[:, kc, :]) for kc in range(16)], reads=[wA.h, xb.h])
                    sw = psn()
                    rope_fm(p_.t[:], p_.h, True, Ct[b], St[b], permb.t[:, 0:128], qb, sw, t1, t2, krst[b].t[:], krst[b].h)
                    P.dma("sp", [(KrT[:, T0:T0 + TT], krst[b].t[:])], krst[b].ld, reads=[krst[b].h], accw=[hK["KrT"]])
                P.flush()

        def phase_B(tiles):
            with ExitStack() as st:
                wB = sb(st, "wB", [128, 16, 2048], BF16, dma="sw")
                xbs = [sb(st, f"xbB{i}", [128, 16, TT], BF16, dma="sw") for i in range(2)]
                dkst = [sb(st, f"dkst{i}", [128, 8, TT], BF16, dma=True) for i in range(2)]
                dvst = [sb(st, f"dvst{i}", [128, 8, 4, 128], BF16, dma=True) for i in range(2)]
                Ct = [sb(st, f"CtB{i}", [128, TT], F32, dma=True) for i in range(2)]
                St = [sb(st, f"StB{i}", [128, TT], F32, dma=True) for i in range(2)]
                qb = [sb(st, f"qbB{i}", [128, TT], BF16) for i in range(2)]
                t1 = [sb(st, f"t1B{i}", [128, TT], F32) for i in range(2)]
                t2 = [sb(st, f"t2B{i}", [128, TT], F32) for i in range(2)]
                P.dma("pool", [(wB.t[:], w_in.rearrange("(kc p) n -> p kc n", p=128)[:, :, O_DK:O_DK + 2048])], wB.ld, writes=[wB.h])
                load_x(xbs[0], tiles[0] * TT)
                for it, tix in enumerate(tiles):
                    T0 = tix * TT
                    b = it % 2
                    xb = xbs[b]
                    if it + 1 < len(tiles):
                        load_x(xbs[(it + 1) % 2], tiles[it + 1] * TT)
                    P.dma("sp", [(Ct[b].t[:], ropeD_C[:, T0:T0 + TT]), (St[b].t[:], ropeD_S[:, T0:T0 + TT])],
                          Ct[b].ld, writes=[Ct[b].h, St[b].h])
                    for h in range(8):
                        p_ = psn()
                        mm_group(P, p_, [(p_.t[:], wB.t[:, kc, h * 128:(h + 1) * 128], xb.t[:, kc, :]) for kc in range(16)],
                                 reads=[wB.h, xb.h])
                        sw = psn()
                        rope_fm(p_.t[:], p_.h, True, Ct[b], St[b], permb.t[:, 128:256], qb[h % 2], sw, t1[h % 2], t2[h % 2],
                                dkst[b].t[:, h, :], dkst[b].h)
                    P.dma("sp", [(DkT.rearrange("h p n -> p h n")[:, :, T0:T0 + TT], dkst[b].t[:])], dkst[b].ld,
                          reads=[dkst[b].h], accw=[hK["DkT"]])
                    for tb in range(4):
                        for half in range(2):
                            p_ = psn()
                            mm_group(P, p_, [(p_.t[:], xb.t[:, kc, tb * 128:(tb + 1) * 128],
                                              wB.t[:, kc, 1024 + half * 512:1024 + (half + 1) * 512]) for kc in range(16)],
                                     reads=[wB.h, xb.h])
                            eng = "act" if (tb + half) % 2 == 0 else "dve"
                            if eng == "act":
                                P.op("act", lambda e, p_=p_, tb=tb, half=half, b=b: e.activation(
                                    out=dvst[b].t[:, half * 4:(half + 1) * 4, tb, :],
                                    in_=p_.t[:].rearrange("p (h d) -> p h d", d=128), func=AF.Copy),
                                     reads=[p_.h], writes=[dvst[b].h])
                            else:
                                P.op("dve", lambda e, p_=p_, tb=tb, half=half, b=b: e.tensor_copy(
                                    out=dvst[b].t[:, half * 4:(half + 1) * 4, tb, :],
                                    in_=p_.t[:].rearrange("p (h d) -> p h d", d=128)),
                                     reads=[p_.h], writes=[dvst[b].h])
                    ch, k0 = T0 // 1024, (T0 % 1024) // 128
                    P.dma("sp", [(Dv[:, ch, :, k0:k0 + 4, :].rearrange("h p k d -> p h k d"), dvst[b].t[:])], dvst[b].ld,
                          reads=[dvst[b].h], accw=[hK["Dv"]])
                P.flush()

        def phase_C(tiles):
            with ExitStack() as st:
                wq = sb(st, "wqa", [128, 16, 768], BF16, dma="sw")
                wqb = sb(st, "wqb", [128, 6, 1536], BF16)
                wst = [sb(st, f"wstC{i}", [128, 1536], F32, dma=True) for i in range(2)]
                xbs = [sb(st, f"xbC{i}", [128, 16, TT], BF16, dma="sw") for i in range(2)]
                qab = [sb(st, f"qab{i}", [128, 6, TT], BF16) for i in range(2)]
                sq = [sb(st, f"sqC{i}", [128, 6, TT], BF16) for i in range(2)]
                rstdB = [sb(st, f"rstdBC{i}", [128, TT], F32) for i in range(2)]
                qnst = [sb(st, f"qnst{i}", [128, 8, TT], BF16, dma=True) for i in range(2)]
                qrst = [sb(st, f"qrst{i}", [128, 4, TT], BF16, dma=True) for i in range(2)]
                Ct = [sb(st, f"CtC{i}", [128, TT], F32, dma=True) for i in range(2)]
                St = [sb(st, f"StC{i}", [128, TT], F32, dma=True) for i in range(2)]
                qf = [sb(st, f"qfC{i}", [128, TT], F32) for i in range(2)]
                qb = [sb(st, f"qbC{i}", [128, TT], BF16) for i in range(2)]
                t1 = [sb(st, f"t1C{i}", [128, TT], F32) for i in range(2)]
                t2 = [sb(st, f"t2C{i}", [128, TT], F32) for i in range(2)]
                P.dma("pool", [(wq.t[:], w_in.rearrange("(kc p) n -> p kc n", p=128)[:, :, 0:768])], wq.ld, writes=[wq.h])
                wqb_v = w_qb.rearrange("(kc p) (h d) -> kc p h d", p=128, d=192)
                for kc in range(6):
                    s_ = wst[kc % 2]
                    P.dma("sp", [(s_.t[:, 0:1024].rearrange("p (h d) -> p h d", d=128), wqb_v[kc, :, :, 0:128]),
                                 (s_.t[:, 1024:1536].rearrange("p (h d) -> p h d", d=64), wqb_v[kc, :, :, 128:192])],
                          s_.ld, writes=[s_.h])
                    P.op("act", lambda e, s_=s_, kc=kc: e.activation(out=wqb.t[:, kc, :], in_=s_.t[:], func=AF.Identity,
                                                                      scale=gq.t[:, kc:kc + 1]),
                         reads=[s_.h, gq.h], writes=[wqb.h])
                load_x(xbs[0], tiles[0] * TT)
                for it, tix in enumerate(tiles):
                    T0 = tix * TT
                    Q0 = it * TT
                    b = it % 2
                    xb = xbs[b]
                    if it + 1 < len(tiles):
                        load_x(xbs[(it + 1) % 2], tiles[it + 1] * TT)
                    P.dma("sp", [(Ct[b].t[:], ropeM_C[:, T0:T0 + TT]), (St[b].t[:], ropeM_S[:, T0:T0 + TT])],
                          Ct[b].ld, writes=[Ct[b].h, St[b].h])
                    for oc in range(6):
                        p_ = psn()
                        mm_group(P, p_, [(p_.t[:], wq.t[:, kc, oc * 128:(oc + 1) * 128], xb.t[:, kc, :]) for kc in range(16)],
                                 reads=[wq.h, xb.h])
                        P.op("act", lambda e, p_=p_, oc=oc, b=b: e.activation(out=qab[b].t[:, oc, :], in_=p_.t[:], func=AF.Copy),
                             reads=[p_.h], writes=[qab[b].h])
                        P.op("act", lambda e, p_=p_, oc=oc, b=b: e.activation(out=sq[b].t[:, oc, :], in_=p_.t[:], func=AF.Square),
                             reads=[p_.h], writes=[sq[b].h])
                    p_ = psn()
                    mm_group(P, p_, [(p_.t[:], ones.t[:], sq[b].t[:, oc, :]) for oc in range(6)], reads=[ones.h, sq[b].h])
                    rsqrt_eps(rstdB[b], rstdB[b].t[:], p_.t[:], [p_.h], 768.0 * RMS_EPS)
                    for h in range(8):
                        p_ = psn()
                        mm_group(P, p_, [(p_.t[:], wqb.t[:, c6, h * 128:(h + 1) * 128], qab[b].t[:, c6, :]) for c6 in range(6)],
                                 reads=[wqb.h, qab[b].h])
                        P.op("dve", lambda e, p_=p_, h=h, b=b: e.tensor_tensor(out=qnst[b].t[:, h, :], in0=p_.t[:], in1=rstdB[b].t[:],
                                                                               op=ALU.mult),
                             reads=[p_.h, rstdB[b].h], writes=[qnst[b].h])
                    P.dma("sp", [(QnT.rearrange("h p n -> p h n")[:, :, Q0:Q0 + TT], qnst[b].t[:])], qnst[b].ld,
                          reads=[qnst[b].h], accw=[hK["QnT"]])
                    for pr in range(4):
                        p_ = psn()
                        mm_group(P, p_, [(p_.t[:], wqb.t[:, c6, 1024 + pr * 128:1024 + (pr + 1) * 128], qab[b].t[:, c6, :])
                                         for c6 in range(6)], reads=[wqb.h, qab[b].h])
                        f_ = qf[pr % 2]
                        P.op("dve", lambda e, p_=p_, f_=f_, b=b: e.tensor_tensor(out=f_.t[:], in0=p_.t[:], in1=rstdB[b].t[:], op=ALU.mult),
                             reads=[p_.h, rstdB[b].h], writes=[f_.h])
                        sw = psn()
                        rope_fm(f_.t[:], f_.h, False, Ct[b], St[b], permb.t[:, 0:128], qb[pr % 2], sw, t1[pr % 2], t2[pr % 2],
                                qrst[b].t[:, pr, :], qrst[b].h)
                    P.dma("sp", [(QrT.rearrange("h p n -> p h n")[:, :, Q0:Q0 + TT], qrst[b].t[:])], qrst[b].ld,
                          reads=[qrst[b].h], accw=[hK["QrT"]])
                P.flush()

        def phase_D(tiles):
            with ExitStack() as st:
                wd = sb(st, "wdq", [128, 16, 1024], BF16, dma="sw")
                xbs = [sb(st, f"xbD{i}", [128, 16, TT], BF16, dma="sw") for i in range(2)]
                dqst = [sb(st, f"dqst{i}", [128, 8, TT], BF16, dma=True) for i in range(2)]
                Ct = [sb(st, f"CtD{i}", [128, TT], F32, dma=True) for i in range(2)]
                St = [sb(st, f"StD{i}", [128, TT], F32, dma=True) for i in range(2)]
                qb = [sb(st, f"qbD{i}", [128, TT], BF16) for i in range(2)]
                t1 = [sb(st, f"t1D{i}", [128, TT], F32) for i in range(2)]
                t2 = [sb(st, f"t2D{i}", [128, TT], F32) for i in range(2)]
                P.dma("pool", [(wd.t[:], w_in.rearrange("(kc p) n -> p kc n", p=128)[:, :, O_DQ:O_DQ + 1024])], wd.ld, writes=[wd.h])
                load_x(xbs[0], tiles[0] * TT)
                for it, tix in enumerate(tiles):
                    T0 = tix * TT
                    Q0 = it * TT
                    b = it % 2
                    xb = xbs[b]
                    if it + 1 < len(tiles):
                        load_x(xbs[(it + 1) % 2], tiles[it + 1] * TT)
                    P.dma("sp", [(Ct[b].t[:], ropeD_C[:, T0:T0 + TT]), (St[b].t[:], ropeD_S[:, T0:T0 + TT])],
                          Ct[b].ld, writes=[Ct[b].h, St[b].h])
                    for h in range(8):
                        p_ = psn()
                        mm_group(P, p_, [(p_.t[:], wd.t[:, kc, h * 128:(h + 1) * 128], xb.t[:, kc, :]) for kc in range(16)],
                                 reads=[wd.h, xb.h])
                        sw = psn()
                        rope_fm(p_.t[:], p_.h, True, Ct[b], St[b], permb.t[:, 128:256], qb[h % 2], sw, t1[h % 2], t2[h % 2],
                                dqst[b].t[:, h, :], dqst[b].h)
                    P.dma("sp", [(DqT.rearrange("h p n -> p h n")[:, :, Q0:Q0 + TT], dqst[b].t[:])], dqst[b].ld,
                          reads=[dqst[b].h], accw=[hK["DqT"]])
                P.flush()

        def conv_jobs():
            jobs = []
            for kc in range(16):
                r0 = kc * 128
                for half in range(2):
                    jobs.append((w_in[r0:r0 + 128, O_G + half * 2048:O_G + (half + 1) * 2048],
                                 Wg[half * 4:(half + 1) * 4, :, kc, :].rearrange("n p j -> p n j")))
            for kc in range(8):
                r0 = kc * 128
                jobs.append((w_brm[r0:r0 + 128, :], Wbm[:, :, kc, :].rearrange("n p j -> p n j")))
                jobs.append((w_brd[r0:r0 + 128, :], Wbd[:, :, kc, :].rearrange("n p j -> p n j")))
            for kc in range(16):
                r0 = kc * 128
                jobs.append((w_out[r0:r0 + 128, :], Wo[:, :, kc, :].rearrange("n p j -> p n j")))
                for q4 in range(4):
                    jobs.append((w_ff1[r0:r0 + 128, q4 * 2048:(q4 + 1) * 2048],
                                 W1[q4 * 4:(q4 + 1) * 4, :, kc, :].rearrange("n p j -> p n j")))
            W2v = W2.rearrange("(ng kg) p kc j -> kg ng p kc j", kg=4)
            for kk in range(64):
                r0 = kk * 128
                jobs.append((w_ff2[r0:r0 + 128, :], W2v[kk // 16, :, :, kk % 16, :].rearrange("n p j -> p n j")))
            return jobs

        cjobs = conv_jobs()
        cpos = [0]

        class Conv:
            def __init__(self, st, tag):
                self.s32 = [sb(st, f"cv32{tag}{i}", [128, 2048], F32, dma=True) for i in range(2)]
                self.sbf = [sb(st, f"cvbf{tag}{i}", [128, 2048], BF16, dma=True) for i in range(2)]
                self.pend = None

            def _store(self):
                if self.pend is not None:
                    b_, dst = self.pend
                    P.dma("sp", [(dst, b_.t[:].rearrange("p (n j) -> p n j", j=dst.shape[-1]))], b_.ld,
                          reads=[b_.h], accw=[hK["W"]])
                    self.pend = None

            def steps(self, n):
                for _ in range(n):
                    if cpos[0] >= len(cjobs):
                        break
                    i = cpos[0] % 2
                    src_ap, dst = cjobs[cpos[0]]
                    cpos[0] += 1
                    a_, b_ = self.s32[i], self.sbf[i]
                    P.dma("sp", [(a_.t[:], src_ap)], a_.ld, writes=[a_.h])
                    self._store()
                    P.op("act", lambda e, a_=a_, b_=b_: e.activation(out=b_.t[:], in_=a_.t[:], func=AF.Copy),
                         reads=[a_.h], writes=[b_.h])
                    self.pend = (b_, dst)

            def finish(self):
                self._store()

        def weight_convert(st):
            stg = [sb(st, f"wcv{i}", [128, 2048], BF16, dma="sw") for i in range(3)]
            stq = [P.newsem(sw=True) for _ in range(3)]
            for n_, (src_ap, dst_ap) in enumerate(cjobs):
                s_ = stg[n_ % 3]
                P.dma("pool", [(s_.t[:], src_ap)], s_.ld, writes=[s_.h])
                P.dma("pool", [(dst_ap, s_.t[:].rearrange("p (n j) -> p n j", j=dst_ap.shape[-1]))], stq[n_ % 3],
                      reads=[s_.h], accw=[hK["W"]])

        def phase_E(groups, units):
            with ExitStack() as st:
                weight_convert(st)
                NR = 3
                qA = [sb(st, f"qA{i}", [128, TT], BF16, dma=True) for i in range(2)]
                qZ = [[sb(st, f"qZ{par}{i}", [128, TT], BF16, dma=True) for i in range(2)] for par in range(2)]
                for par in range(2):
                    for i in range(2):
                        P.op("dve", lambda e, z=qZ[par][i]: e.memset(z.t[:], 0.0), writes=[qZ[par][i].h])
                kA = [sb(st, f"kA{i}", [128, 1024], BF16, dma=True) for i in range(NR)]
                kB = [sb(st, f"kB{i}", [128, 1024], BF16, dma=True) for i in range(NR)]
                vv = [sb(st, f"vv{i}", [128, 8, 128], BF16, dma=True) for i in range(NR)]
                NPT = 4
                ptt = [st.enter_context(nc.sbuf_tensor(f"sb_ptt{i}", [128, 2, TT], BF16)) for i in range(NPT)]
                pt0 = [View(ptt[i][:, 0, :]) for i in range(NPT)]
                pt1 = [View(ptt[i][:, 1, :]) for i in range(NPT)]
                acc2 = [st.enter_context(nc.sbuf_tensor(f"sb_acc{i}", [128, 2, TT], F32)) for i in range(2)]
                accD = [View(acc2[i][:, 0, :]) for i in range(2)]
                accP = [View(acc2[i][:, 1, :]) for i in range(2)]
                ones32 = sb(st, "ones32", [128, 128], F32)
                rl = sb(st, "rl", [128, TT], F32)
                ob = [sb(st, f"ob{i}", [128, TT], BF16, dma=True) for i in range(2)]
                of = [sb(st, f"of{i}", [128, TT], F32, dma=True) for i in range(2)]
                P.op("dve", lambda e: e.memset(ones32.t[:], 1.0), writes=[ones32.h])
                O_ = [ps[4], ps[5]]
                L_ = [ps[6], ps[7]]
                uq = 0
                chn = 0
                pti = 0
                gpi = 0
                dfin = [None]
                dstore = [None]

                def run(slot):
                    if slot[0] is not None:
                        f_ = slot[0]
                        slot[0] = None
                        f_()

                for (q_lo, nqt, k_lo, k_len) in groups:
                    nch = k_len // 1024
                    npairs = nch * 4
                    for (kind, h, m) in units:
                        for j in range(nqt):
                            q0 = q_lo + j * TT
                            Oa, La, aD, aP = O_[uq % 2], L_[uq % 2], accD[uq % 2], accP[uq % 2]
                            if kind == "mla":
                                pb = (h % 2) * 64
                                qa_, qb_ = qA[uq % 2], qZ[h % 2][uq % 2]
                                P.dma("sp", [(qa_.t[:], QnT[h, :, q0:q0 + TT]),
                                             (qb_.t[pb:pb + 64, :], QrT[h // 2, pb:pb + 64, q0:q0 + TT])], qa_.ld,
                                      reads=[hK["QnT"], hK["QrT"]], writes=[qa_.h, qb_.h])
                                scale = 192.0 ** -0.5
                            else:
                                pb = m * 64
                                qa_ = qZ[m][uq % 2]
                                qb_ = qa_
                                P.dma("sp", [(qa_.t[pb:pb + 64, :], DqT[h, pb:pb + 64, q0:q0 + TT])], qa_.ld, reads=[hK["DqT"]],
                                      writes=[qa_.h])
                                scale = 64.0 ** -0.5
                            pending = None

                            def emit_pv(pending, Oa=Oa):
                                p0, p1, vb, ppr, first, last = pending

                                def fpv(pe):
                                    pe.matmul(Oa.t[:], vb.t[:, 2 * ppr, :], p0.t[:], start=first, stop=False)
                                    return pe.matmul(Oa.t[:], vb.t[:, 2 * ppr + 1, :], p1.t[:], start=False, stop=last)
                                P.op("pe", fpv, reads=[p0.h, p1.h, vb.h], writes=[Oa.h])

                            for c in range(nch):
                                k0 = k_lo + c * 1024
                                ci = chn % NR
                                chn += 1
                                ka_, kb_, v_ = kA[ci], kB[ci], vv[ci]
                                chx = k0 // 1024
                                if kind == "mla":
                                    P.dma("sp", [(ka_.t[:], KnT[h, :, k0:k0 + 1024]), (kb_.t[:], KrT[:, k0:k0 + 1024]),
                                                 (v_.t[:], Vm[h, chx])], ka_.ld,
                                          reads=[hK["KnT"], hK["KrT"], hK["Vm"]], writes=[ka_.h, kb_.h, v_.h])
                                else:
                                    P.dma("sp", [(ka_.t[:], DkT[h, :, k0:k0 + 1024]), (v_.t[:], Dv[h, chx])], ka_.ld,
                                          reads=[hK["DkT"], hK["Dv"]], writes=[ka_.h, v_.h])
                                if c == min(1, nch - 1):
                                    run(dstore)
                                for pr in range(4):
                                    gi = c * 4 + pr
                                    sbi = gpi % 2
                                    gpi += 1
                                    Sbig = psbig[sbi]
                                    hS = [ps[2 * sbi].h, ps[2 * sbi + 1].h]

                                    def fqk(pe, Sbig=Sbig, ka_=ka_, kb_=kb_, qa_=qa_, qb_=qb_, pr=pr, kind=kind):
                                        ins = None
                                        for t in range(2):
                                            kt = 2 * pr + t
                                            ks = slice(kt * 128, (kt + 1) * 128)
                                            o_ = Sbig[:, t * 512:(t + 1) * 512]
                                            if kind == "mla":
                                                pe.matmul(o_, ka_.t[:, ks], qa_.t[:], start=True, stop=False)
                                                ins = pe.matmul(o_, kb_.t[:, ks], qb_.t[:], start=False, stop=True)
                                            else:
                                                ins = pe.matmul(o_, ka_.t[:, ks], qa_.t[:], start=True, stop=True)
                                        return ins
                                    rd = [ka_.h, kb_.h, qa_.h, qb_.h] if kind == "mla" else [ka_.h, qa_.h]
                                    P.op("pe", fqk, reads=rd, writes=hS)
                                    if gi == 0:
                                        run(dfin)
                                    if pending is not None:
                                        emit_pv(pending)
                                    p0, p1, pfull = pt0[pti % NPT], pt1[pti % NPT], ptt[pti % NPT]
                                    pti += 1
                                    P.op("act", lambda e, pfull=pfull, Sbig=Sbig, scale=scale: e.activation(
                                        out=pfull[:].rearrange("p a n -> p (a n)"), in_=Sbig[:], func=AF.Exp, scale=scale),
                                         reads=[], writes=[p0.h, p1.h] + hS)
                                    ac_ap = acc2[uq % 2][:].rearrange("p a n -> p (a n)")
                                    pf_ap = pfull[:].rearrange("p a n -> p (a n)")
                                    if gi == 0:
                                        P.op("dve", lambda e, ac_ap=ac_ap, pf_ap=pf_ap: e.tensor_copy(out=ac_ap, in_=pf_ap),
                                             reads=[p0.h, p1.h], writes=[aD.h, aP.h])
                                    else:
                                        P.op("dve", lambda e, ac_ap=ac_ap, pf_ap=pf_ap: e.tensor_tensor(out=ac_ap, in0=ac_ap, in1=pf_ap, op=ALU.add),
                                             reads=[p0.h, p1.h, aD.h, aP.h], writes=[aD.h, aP.h])
                                    pending = (p0, p1, v_, pr, gi == 0, gi == npairs - 1)
                            emit_pv(pending)

                            def fin(La=La, Oa=Oa, aD=aD, aP=aP, kind=kind, h=h, m=m, q0=q0, uqi=uq):
                                def fl(pe):
                                    pe.matmul(La.t[:], ones32.t[:], aD.t[:], start=True, stop=False)
                                    return pe.matmul(La.t[:], ones32.t[:], aP.t[:], start=False, stop=True)
                                P.op("pe", fl, reads=[ones32.h, aD.h, aP.h], writes=[La.h])
                                P.op("dve", lambda e: e.reciprocal(out=rl.t[:], in_=La.t[:]), reads=[La.h], writes=[rl.h])
                                o_ = ob[uqi % 2] if kind == "mla" else of[uqi % 2]
                                P.op("dve", lambda e: e.tensor_tensor(out=o_.t[:], in0=Oa.t[:], in1=rl.t[:], op=ALU.mult),
                                     reads=[Oa.h, rl.h], writes=[o_.h])

                                def sto():
                                    if kind == "mla":
                                        P.dma("sp", [(OmT[h, :, q0:q0 + TT], o_.t[:])], o_.ld, reads=[o_.h], accw=[hK["OmT"]])
                                    else:
                                        P.dma("sp", [(OdT[h * 2 + m, :, q0:q0 + TT], o_.t[:])], o_.ld, reads=[o_.h], accw=[hK["OdT"]])
                                run(dstore)
                                dstore[0] = sto
                            run(dfin)
                            dfin[0] = fin
                            uq += 1
                run(dfin)
                run(dstore)
                P.flush()

        def phase_F(tiles):
            with ExitStack() as st:
                A_t = st.enter_context(nc.sbuf_tensor("sb_A", [128, 32, TT], BF16))
                A = [View(A_t[:, i, :]) for i in range(32)]
                r = [sb(st, f"r{i}", [128, TT], F32) for i in range(16)]
                x1b = [sb(st, f"x1b{i}", [128, TT], BF16) for i in range(16)]
                wp = [sb(st, f"wp{i}", [128, 16, 512], BF16, dma=True) for i in range(3)]
                xf = [sb(st, f"xf{i}", [128, TT], F32, dma=True) for i in range(3)]
                od0 = [sb(st, f"od0{i}", [128, TT], F32, dma=True) for i in range(2)]
                od1 = [sb(st, f"od1{i}", [128, TT], F32, dma=True) for i in range(2)]
                df = [sb(st, f"df{i}", [128, TT], F32) for i in range(2)]
                sqd = [sb(st, f"sqd{i}", [128, TT], BF16) for i in range(2)]
                rsd = [sb(st, f"rsd{i}", [128, TT], F32) for i in range(2)]
                gm = [sb(st, f"gm{i}", [128, TT], F32) for i in range(2)]
                gd = [sb(st, f"gd{i}", [128, TT], F32) for i in range(2)]
                ta = [sb(st, f"ta{i}", [128, TT], F32) for i in range(2)]
                tb_ = [sb(st, f"tb{i}", [128, TT], F32) for i in range(2)]
                rb = [sb(st, f"rb{i}", [128, TT], BF16) for i in range(6)]
                rq = [sb(st, f"rq{i}", [128, TT], BF16) for i in range(6)]
                meanB = sb(st, "meanB", [128, TT], F32)
                msq = sb(st, "msq", [128, TT], F32)
                rstdB = sb(st, "rstdBF", [128, TT], F32)
                yst = [P.newsem() for _ in range(2)]
                xld = P.newsem(sw=True)
                omld = P.newsem()
                wpi = [0]
                xfi = [0]
                xb = A[0:16]
                om = A[16:24]
                odn = A[24:32]
                mg = x1b

                def load_panel(src_ap, dst_sl=None, w_=None):
                    if w_ is None:
                        w_ = wp[wpi[0] % 3]
                        wpi[0] += 1
                    dst = w_.t[:] if dst_sl is None else w_.t[:, dst_sl, :]
                    P.dma("sp", [(dst, src_ap)], w_.ld, reads=[hK["W"]], writes=[w_.h])
                    return w_

                def layer_norm(gcol0, bcol0, want_bf16, S1, S2):
                    P.op("dve", lambda e: e.tensor_scalar(out=meanB.t[:], in0=S1.t[:], scalar1=1.0 / D, scalar2=None, op0=ALU.mult),
                         reads=[S1.h], writes=[meanB.h])
                    P.op("dve", lambda e: e.tensor_tensor(out=msq.t[:], in0=meanB.t[:], in1=meanB.t[:], op=ALU.mult),
                         reads=[meanB.h], writes=[msq.h])
                    P.op("dve", lambda e: e.scalar_tensor_tensor(out=rstdB.t[:], in0=S2.t[:], scalar=1.0 / D, in1=msq.t[:],
                                                                 op0=ALU.mult, op1=ALU.subtract),
                         reads=[S2.h, msq.h], writes=[rstdB.h])
                    rsqrt_eps(rstdB, rstdB.t[:], rstdB.t[:], [rstdB.h], LN_EPS)
                    for oc in range(16):
                        r_ = r[oc]
                        P.op("dve", lambda e, r_=r_: e.tensor_tensor(out=r_.t[:], in0=r_.t[:], in1=meanB.t[:], op=ALU.subtract),
                             reads=[r_.h, meanB.h], writes=[r_.h])
                        P.op("dve", lambda e, r_=r_: e.tensor_tensor(out=r_.t[:], in0=r_.t[:], in1=rstdB.t[:], op=ALU.mult),
                             reads=[r_.h, rstdB.h], writes=[r_.h])
                        P.op("act", lambda e, r_=r_, oc=oc: e.activation(out=r_.t[:], in_=r_.t[:], func=AF.Identity,
                                                                          scale=vecs.t[:, gcol0 + oc:gcol0 + oc + 1],
                                                                          bias=vecs.t[:, bcol0 + oc:bcol0 + oc + 1]),
                             reads=[r_.h, vecs.h], writes=[r_.h])
                        if want_bf16:
                            P.op("act", lambda e, r_=r_, oc=oc: e.activation(out=x1b[oc].t[:], in_=r_.t[:], func=AF.Copy),
                                 reads=[r_.h], writes=[x1b[oc].h])

                sdef = []

                def flush_stats():
                    while sdef:
                        sdef.pop(0)()

                def stats(oc, S1, S2):
                    r_ = r[oc]
                    b_, q_ = rb[oc % 6], rq[oc % 6]
                    P.op("act", lambda e: e.activation(out=b_.t[:], in_=r_.t[:], func=AF.Copy), reads=[r_.h], writes=[b_.h])
                    P.op("act", lambda e: e.activation(out=q_.t[:], in_=r_.t[:], func=AF.Square), reads=[r_.h], writes=[q_.h])

                    def fst(pe):
                        pe.matmul(S1.t[:], ones.t[:], b_.t[:], start=(oc == 0), stop=(oc == 15))
                        return pe.matmul(S2.t[:], ones.t[:], q_.t[:], start=(oc == 0), stop=(oc == 15))

                    def emit():
                        P.op("pe", fst, reads=[ones.h, b_.h, q_.h], writes=[S1.h, S2.h])
                    sdef.append(emit)

                for it, tix in enumerate(tiles):
                    T0 = tix * TT
                    Q0 = it * TT
                    P.dma("pool", [(A_t[:, 0:16, :], xT_v[:, :, T0:T0 + TT])], xld, writes=[xb[kc].h for kc in range(16)])
                    P.dma("sp", [(A_t[:, 16:24, :], OmT.rearrange("h p n -> p h n")[:, :, Q0:Q0 + TT])], omld, reads=[hK["OmT"]],
                          writes=[om[h].h for h in range(8)])
                    for h in range(8):
                        i2 = h % 2
                        P.dma("sp", [(od0[i2].t[:], OdT[2 * h, :, Q0:Q0 + TT]), (od1[i2].t[:], OdT[2 * h + 1, :, Q0:Q0 + TT])],
                              od0[i2].ld, reads=[hK["OdT"]], writes=[od0[i2].h, od1[i2].h])
                        P.op("dve", lambda e, i2=i2: e.scalar_tensor_tensor(out=df[i2].t[:], in0=od1[i2].t[:], scalar=lamc.t[:, 5:6],
                                                                            in1=od0[i2].t[:], op0=ALU.mult, op1=ALU.add),
                             reads=[od0[i2].h, od1[i2].h, lamc.h], writes=[df[i2].h])
                        P.op("act", lambda e, i2=i2: e.activation(out=sqd[i2].t[:], in_=df[i2].t[:], func=AF.Square),
                             reads=[df[i2].h], writes=[sqd[i2].h])
                        p_ = psn()
                        mm_group(P, p_, [(p_.t[:], ones.t[:], sqd[i2].t[:])], reads=[ones.h, sqd[i2].h])
                        rsqrt_eps(rsd[i2], rsd[i2].t[:], p_.t[:], [p_.h], 128.0 * RMS_EPS)
                        P.op("dve", lambda e, i2=i2, h=h: e.scalar_tensor_tensor(out=odn[h].t[:], in0=df[i2].t[:], scalar=lamc.t[:, 6:7],
                                                                                 in1=rsd[i2].t[:], op0=ALU.mult, op1=ALU.mult),
                             reads=[df[i2].h, rsd[i2].h, lamc.h], writes=[odn[h].h])
                    for ocg in range(4):
                        wgm = load_panel(Wg[ocg])
                        wgd = load_panel(Wg[4 + ocg])
                        c0 = (ocg % 2) * 512
                        wbr = load_panel(Wbm[ocg // 2, :, :, c0:c0 + 512], slice(0, 8))
                        load_panel(Wbd[ocg // 2, :, :, c0:c0 + 512], slice(8, 16), w_=wbr)
                        for o4 in range(4):
                            oc = ocg * 4 + o4
                            pm, pd, pgm, pgd = psn(), psn(), psn(), psn()
                            cs = slice(o4 * 128, (o4 + 1) * 128)
                            mm_group(P, pgm, [(pgm.t[:], wgm.t[:, kc, cs], xb[kc].t[:]) for kc in range(16)],
                                     reads=[wgm.h] + [xb[kc].h for kc in range(16)])
                            mm_group(P, pgd, [(pgd.t[:], wgd.t[:, kc, cs], xb[kc].t[:]) for kc in range(16)],
                                     reads=[wgd.h] + [xb[kc].h for kc in range(16)])
                            mm_group(P, pm, [(pm.t[:], wbr.t[:, k, cs], om[k].t[:]) for k in range(8)],
                                     reads=[wbr.h] + [om[k].h for k in range(8)])
                            mm_group(P, pd, [(pd.t[:], wbr.t[:, 8 + k, cs], odn[k].t[:]) for k in range(8)],
                                     reads=[wbr.h] + [odn[k].h for k in range(8)])
                            i2 = oc % 2
                            P.op("act", lambda e, pgm=pgm, i2=i2, oc=oc: e.activation(out=gm[i2].t[:], in_=pgm.t[:], func=AF.Sigmoid,
                                                                                      bias=vecs.t[:, V_BG + oc:V_BG + oc + 1]),
                                 reads=[pgm.h, vecs.h], writes=[gm[i2].h])
                            P.op("act", lambda e, pgd=pgd, i2=i2, oc=oc: e.activation(out=gd[i2].t[:], in_=pgd.t[:], func=AF.Sigmoid,
                                                                                      bias=vecs.t[:, V_BG + 16 + oc:V_BG + 16 + oc + 1]),
                                 reads=[pgd.h, vecs.h], writes=[gd[i2].h])
                            P.op("dve", lambda e, pm=pm, i2=i2: e.tensor_tensor(out=ta[i2].t[:], in0=pm.t[:], in1=gm[i2].t[:], op=ALU.mult),
                                 reads=[pm.h, gm[i2].h], writes=[ta[i2].h])
                            P.op("dve", lambda e, pd=pd, i2=i2: e.tensor_tensor(out=tb_[i2].t[:], in0=pd.t[:], in1=gd[i2].t[:], op=ALU.mult),
                                 reads=[pd.h, gd[i2].h], writes=[tb_[i2].h])
                            P.op("dve", lambda e, i2=i2, oc=oc: e.tensor_tensor(out=mg[oc].t[:], in0=ta[i2].t[:], in1=tb_[i2].t[:], op=ALU.add),
                                 reads=[ta[i2].h, tb_[i2].h], writes=[mg[oc].h])
                    S1, S2 = ps[6], ps[7]
                    for ocg in range(4):
                        wo_ = load_panel(Wo[ocg])
                        for o4 in range(4):
                            oc = ocg * 4 + o4
                            acc = ps[(oc % 4)]
                            mm_group(P, acc, [(acc.t[:], wo_.t[:, kc, o4 * 128:(o4 + 1) * 128], mg[kc].t[:]) for kc in range(16)],
                                     reads=[wo_.h] + [mg[kc].h for kc in range(16)])
                            flush_stats()
                            x_ = xf[xfi[0] % 3]
                            xfi[0] += 1
                            P.dma("sp", [(x_.t[:], xT[oc * 128:(oc + 1) * 128, T0:T0 + TT])], x_.ld, writes=[x_.h])
                            r_ = r[oc]
                            P.op("dve", lambda e, r_=r_, x_=x_, acc=acc: e.scalar_tensor_tensor(
                                out=r_.t[:], in0=x_.t[:], scalar=ALPHA, in1=acc.t[:], op0=ALU.mult, op1=ALU.add),
                                 reads=[x_.h, acc.h], writes=[r_.h])
                            stats(oc, S1, S2)
                    flush_stats()
                    layer_norm(V_L1G, V_L1B, True, S1, S2)
                    for hf in range(2):
                        for pn in range(hf * 8, hf * 8 + 8):
                            w1_ = load_panel(W1[pn])
                            for o4 in range(4):
                                fc = pn * 4 + o4
                                fl = fc - hf * 32
                                acc = ps[4 + fc % 4] if hf == 1 else ps[fc % 6]
                                mm_group(P, acc, [(acc.t[:], w1_.t[:, kc, o4 * 128:(o4 + 1) * 128], x1b[kc].t[:]) for kc in range(16)],
                                         reads=[w1_.h] + [x1b[kc].h for kc in range(16)])
                                i2 = fc % 2
                                P.op("act", lambda e, acc=acc, i2=i2: e.activation(out=ta[i2].t[:], in_=acc.t[:], func=AF.Relu),
                                     reads=[acc.h], writes=[ta[i2].h])
                                P.op("dve", lambda e, i2=i2, fl=fl: e.tensor_tensor(out=A[fl].t[:], in0=ta[i2].t[:], in1=ta[i2].t[:], op=ALU.mult),
                                     reads=[ta[i2].h], writes=[A[fl].h])
                        for ng in range(4):
                            accs = [ps[0], ps[1], ps[2], ps[3]]
                            for k2 in range(2):
                                kg = hf * 2 + k2
                                w2_ = load_panel(W2[ng * 4 + kg])

                                def f2(pe, w2_=w2_, k2=k2, accs=accs):
                                    ins = None
                                    for o4 in range(4):
                                        for kc in range(16):
                                            ins = pe.matmul(accs[o4].t[:], w2_.t[:, kc, o4 * 128:(o4 + 1) * 128], A[k2 * 16 + kc].t[:],
                                                            start=(k2 == 0 and kc == 0), stop=(k2 == 1 and kc == 15))
                                    return ins
                                P.op("pe", f2, reads=[w2_.h] + [A[k2 * 16 + kc].h for kc in range(16)], writes=[a.h for a in accs])
                            flush_stats()
                            for o4 in range(4):
                                oc = ng * 4 + o4
                                r_, acc = r[oc], accs[o4]
                                if hf == 0:
                                    P.op("dve", lambda e, r_=r_, acc=acc: e.scalar_tensor_tensor(
                                        out=r_.t[:], in0=r_.t[:], scalar=ALPHA, in1=acc.t[:], op0=ALU.mult, op1=ALU.add),
                                         reads=[r_.h, acc.h], writes=[r_.h])
                                else:
                                    P.op("dve", lambda e, r_=r_, acc=acc: e.tensor_tensor(out=r_.t[:], in0=acc.t[:], in1=r_.t[:], op=ALU.add),
                                         reads=[r_.h, acc.h], writes=[r_.h])
                                    stats(oc, S1, S2)
                    flush_stats()
                    layer_norm(V_L2G, V_L2B, False, S1, S2)
                    ys = yst[it % 2]
                    P.dma("sp", [(yT[oc * 128:(oc + 1) * 128, Q0:Q0 + TT], r[oc].t[:]) for oc in range(16)], ys,
                          reads=[r[oc].h for oc in range(16)], accw=[hK["yT"]])
                P.op("sp", None, reads=[hK["yT"]])
                P.flush()

        groups = [(0, 4, 0, SP_LEN), (NQ // 2, 4, SP_LEN, SS_LEN)]
        units = [("mla", h, 0) for h in range(8)] + [("diff", h, m) for h in range(8) for m in range(2)]
        own = OWN_TILES if own is None else own
        if "A" in phases:
            phase_A(list(range(ntk)))
        if "B" in phases:
            phase_B(list(range(ntk)))
        if "C" in phases:
            phase_C(own)
        if "D" in phases:
            phase_D(own)
        if "E" in phases:
            phase_E(groups if egroups is None else egroups, units if eunits is None else eunits)
        if "F" in phases:
            phase_F(own)
        if "F" not in phases:
            P.flush()
    return nc


def _rope_tables(pos):
    pos = pos.astype(np.float32)
    inv_m = (np.float32(ROPE_THETA) ** (-np.arange(0, 64, 2, dtype=np.float32) / np.float32(64))).astype(np.float32)
    ang_m = (pos[None, :] * inv_m[:, None]).astype(np.float32)
    cm, sm = np.cos(ang_m).astype(np.float32), np.sin(ang_m).astype(np.float32)
    C64 = np.concatenate([cm, cm], 0)
    S64 = np.concatenate([-sm, sm], 0)
    MC = np.concatenate([C64, C64], 0)
    MS = np.concatenate([S64, S64], 0)
    inv_d = (np.float32(ROPE_THETA) ** (-np.arange(0, 16, 2, dtype=np.float32) / np.float32(16))).astype(np.float32)
    ang_d = (pos[None, :] * inv_d[:, None]).astype(np.float32)
    cd, sd = np.cos(ang_d).astype(np.float32), np.sin(ang_d).astype(np.float32)
    n = pos.shape[0]
    C1 = np.concatenate([cd, cd, np.ones((48, n), np.float32)], 0)
    S1 = np.concatenate([-sd, sd, np.zeros((48, n), np.float32)], 0)
    DC = np.concatenate([C1, C1], 0)
    DS = np.concatenate([S1, S1], 0)
    return [np.ascontiguousarray(a) for a in (MC, MS, DC, DS)]


def _perms():
    pm = np.zeros((128, 256), np.float32)
    for i in range(128):
        blk, d = (i // 64) * 64, i % 64
        pm[blk + (d + 32) % 64, i] = 1.0
        if d < 16:
            pm[blk + (d + 8) % 16, 128 + i] = 1.0
        else:
            pm[i, 128 + i] = 1.0
    return pm


_NC_CACHE = {}


def kernel(x_prompt, x_sample, w_in, b_gate, g_qa, w_qb, g_kva, w_kvb, lam_q, lam_k, g_sub,
           w_br_mla, w_br_diff, w_out, ln1_g, ln1_b, w_ff1, w_ff2, ln2_g, ln2_b, _debug=False, _bkw=None):
    f = lambda a: np.ascontiguousarray(np.asarray(a, dtype=np.float32))
    xp = f(x_prompt)[0]
    xs = f(x_sample)
    col = lambda v, n: f(v).reshape(n, 128).T
    vecs = np.zeros((128, NV), np.float32)
    vecs[:, V_BG:V_BG + 32] = col(b_gate, 32)
    vecs[:, V_GQA:V_GQA + 6] = col(g_qa, 6)
    vecs[:, V_GKVA:V_GKVA + 4] = col(g_kva, 4)
    vecs[:, V_L1G:V_L1G + 16] = col(ln1_g, 16)
    vecs[:, V_L1B:V_L1B + 16] = col(ln1_b, 16)
    vecs[:, V_L2G:V_L2G + 16] = col(ln2_g, 16)
    vecs[:, V_L2B:V_L2B + 16] = col(ln2_b, 16)
    vecs[:, V_GSUB] = f(g_sub).reshape(128)
    lamqk = np.ascontiguousarray(np.broadcast_to(
        np.concatenate([f(lam_q).reshape(128), f(lam_k).reshape(128)])[None, :], (128, 256)))
    perms = _perms()
    shared = {
        "w_in": f(w_in)[0], "w_qb": f(w_qb)[0], "w_kvb": f(w_kvb)[0], "w_br_mla": f(w_br_mla)[0],
        "w_br_diff": f(w_br_diff)[0], "w_out": f(w_out)[0], "w_ff1": f(w_ff1)[0], "w_ff2": f(w_ff2)[0],
        "vecs": vecs, "lamqk": lamqk, "perms": perms,
    }
    in_maps = []
    for c in range(8):
        own_p = np.arange(2048 * c, 2048 * (c + 1))
        rest_p = np.concatenate([np.arange(0, 2048 * c), np.arange(2048 * (c + 1), SP_LEN)])
        sb_, hf = c // 2, c % 2
        own_s = np.arange(2048 * hf, 2048 * (hf + 1))
        rest_s = np.arange(2048 * (1 - hf), 2048 * (2 - hf))
        pos = np.concatenate([own_p, rest_p, own_s, rest_s])
        xk = np.concatenate([xp[own_p], xp[rest_p], xs[sb_][own_s], xs[sb_][rest_s]], 0)
        MC, MS, DC, DS = _rope_tables(pos)
        m = dict(shared)
        m.update({"xT": np.ascontiguousarray(xk.T), "ropeM_C": MC, "ropeM_S": MS, "ropeD_C": DC, "ropeD_S": DS})
        in_maps.append(m)
    key = bool(_debug)
    if key not in _NC_CACHE:
        _NC_CACHE[key] = build_program(debug=_debug, **(_bkw or {}))
    nc = _NC_CACHE[key]
    if _debug:
        return run_bass_kernel_spmd(nc, in_maps[:1], core_ids=[0]), in_maps[0]
    res = run_bass_kernel_spmd(nc, in_maps, core_ids=list(range(8)))
    y_prompt = np.empty((1, SP_LEN, D), np.float32)
    y_sample = np.empty((4, SS_LEN, D), np.float32)
    for c in range(8):
        yT = np.asarray(res.results[c]["yT"], dtype=np.float32)
        y_prompt[0, 2048 * c:2048 * (c + 1), :] = yT[:, 0:2048].T
        y_sample[c // 2, 2048 * (c % 2):2048 * (c % 2 + 1), :] = yT[:, 2048:4096].T
    return (y_prompt, y_sample)
```
